# Optimizing a Trainium2 kernel written in Bass

```python
import jax, jax.numpy as jnp
from jax import lax
import numpy as np

D_MODEL = 1024
BATCH = 8
SEQ = 4096
DEPTH = 2

A_GROUPS = ((128, 1), (512, 4), (2048, 16))
A_HEADS = 4
A_HEAD_DIM = 128
A_WIDTH = A_HEADS * A_HEAD_DIM
ROPE_THETA = 500000.0
ROPE_DIM = A_HEAD_DIM // 4

B_HEADS = 4
B_KEY_DIM = D_MODEL // 2 // B_HEADS
B_VAL_DIM = D_MODEL // B_HEADS
B_KEY_WIDTH = B_HEADS * B_KEY_DIM
B_VAL_WIDTH = B_HEADS * B_VAL_DIM
GATE_RANK = 16
GATE_NORMALIZER = 16.0
GLA_CHUNK = 64

N_BRANCHES = 2
NORM_EPS = 1e-6
NEG_BIG = -1e30

IN_WIDTHS = (
    3 * len(A_GROUPS) * A_WIDTH,
    A_WIDTH,
    B_KEY_WIDTH,
    B_KEY_WIDTH,
    B_VAL_WIDTH,
    B_VAL_WIDTH,
    N_BRANCHES * D_MODEL,
    GATE_RANK,
    GATE_RANK,
)
IN_DIM = sum(IN_WIDTHS)

kernel_name = "hybrid_dilated_swa_gla_gated_merge"


def rmsnorm(x, w):
    xf = x.astype(jnp.float32)
    y = xf * lax.rsqrt(jnp.mean(xf * xf, axis=-1, keepdims=True) + NORM_EPS)
    return (y * w.astype(jnp.float32)).astype(x.dtype)


def rope_partial(x, pos):
    inv_freq = ROPE_THETA ** (-jnp.arange(0, ROPE_DIM, 2, dtype=jnp.float32) / ROPE_DIM)
    ang = pos.astype(jnp.float32)[:, None] * inv_freq[None, :]
    cos = jnp.cos(ang)[None, :, None, :]
    sin = jnp.sin(ang)[None, :, None, :]
    xr = x[..., :ROPE_DIM].astype(jnp.float32)
    x1, x2 = jnp.split(xr, 2, axis=-1)
    rot = jnp.concatenate([x1 * cos - x2 * sin, x2 * cos + x1 * sin], axis=-1)
    return jnp.concatenate([rot.astype(x.dtype), x[..., ROPE_DIM:]], axis=-1)


def dilated_window_attention(q, k, v, dilation, half_span):
    B, S, H, Dh = q.shape
    L = S // dilation
    blk = half_span
    nb = -(-L // blk)
    Lp = nb * blk

    def to_classes(t):
        return t.reshape(B, L, dilation, H, Dh).transpose(0, 2, 3, 1, 4)

    qc = jnp.pad(to_classes(q), ((0, 0), (0, 0), (0, 0), (0, Lp - L), (0, 0)))
    qc = qc.reshape(B, dilation, H, nb, blk, Dh)

    def key_windows(t):
        tc = jnp.pad(to_classes(t), ((0, 0), (0, 0), (0, 0), (blk, Lp - L + blk), (0, 0)))
        tc = tc.reshape(B, dilation, H, nb + 2, blk, Dh)
        return jnp.concatenate([tc[:, :, :, :-2], tc[:, :, :, 1:-1], tc[:, :, :, 2:]], axis=4)

    kw = key_windows(k)
    vw = key_windows(v)

    s = jnp.einsum('bghnqe,bghnke->bghnqk', qc, kw).astype(jnp.float32) * (Dh ** -0.5)
    n_idx = jnp.arange(nb)[:, None, None]
    tq = n_idx * blk + jnp.arange(blk)[None, :, None]
    tk = (n_idx - 1) * blk + jnp.arange(3 * blk)[None, None, :]
    valid = (jnp.abs(tk - tq) <= half_span) & (tk >= 0) & (tk < L)
    s = jnp.where(valid, s, jnp.float32(NEG_BIG))
    lse = jax.nn.logsumexp(s, axis=-1)
    p = jnp.exp(s - lse[..., None])
    o = jnp.einsum('bghnqk,bghnke->bghnqe', p.astype(v.dtype), vw)

    o = o.reshape(B, dilation, H, Lp, Dh)[:, :, :, :L].transpose(0, 3, 1, 2, 4).reshape(B, S, H, Dh)
    lse = lse.reshape(B, dilation, H, Lp)[:, :, :, :L].transpose(0, 3, 1, 2).reshape(B, S, H)
    return o, lse


def gla_chunked(q, k, v, log_g, strict):
    B, H, S, dk = q.shape
    dv = v.shape[-1]
    C = GLA_CHUNK
    n = S // C
    q = q.reshape(B, H, n, C, dk)
    k = k.reshape(B, H, n, C, dk)
    v = v.reshape(B, H, n, C, dv)
    b = jnp.cumsum(log_g.reshape(B, H, n, C, dk), axis=3)
    q_dec = q * jnp.exp(b)
    k_inv = k * jnp.exp(-b)
    a = jnp.einsum('bhncd,bhnjd->bhncj', q_dec, k_inv)
    mask = jnp.tril(jnp.ones((C, C), dtype=bool), k=-1 if strict else 0)
    o_intra = jnp.einsum('bhncj,bhnje->bhnce', jnp.where(mask, a, 0.0), v)

    b_last = b[:, :, :, -1:, :]
    k_end = k * jnp.exp(b_last - b)
    chunk_decay = jnp.exp(b_last[:, :, :, 0, :])

    def step(state, xs):
        qd, ke, vv, dec = xs
        o = jnp.einsum('bhcd,bhde->bhce', qd, state)
        state = dec[..., None] * state + jnp.einsum('bhcd,bhce->bhde', ke, vv)
        return state, o

    xs = (jnp.moveaxis(q_dec, 2, 0), jnp.moveaxis(k_end, 2, 0),
          jnp.moveaxis(v, 2, 0), jnp.moveaxis(chunk_decay, 2, 0))
    _, o_inter = lax.scan(step, jnp.zeros((B, H, dk, dv), jnp.float32), xs)
    o = o_intra + jnp.moveaxis(o_inter, 0, 2)
    return o.reshape(B, H, S, dv)


def gla_gate(lr, w_up, b_up):
    logits = (jnp.einsum('bsr,rk->bsk', lr, w_up) + b_up).astype(jnp.float32)
    g = jax.nn.log_sigmoid(logits) / GATE_NORMALIZER
    Bsz, S, _ = g.shape
    return g.reshape(Bsz, S, B_HEADS, B_KEY_DIM).transpose(0, 2, 1, 3)


def hybrid_layer(x, pos, norm_pre, w_in, gate_up_fwd, gate_bias_fwd, gate_up_bwd, gate_bias_bwd,
                 gla_out_norm, w_branch_a, w_branch_b, w_out, norm_post):
    Bsz, S, _ = x.shape
    h = rmsnorm(x, norm_pre)
    proj = jnp.einsum('bsd,de->bse', h, w_in)
    splits = [int(c) for c in np.cumsum(IN_WIDTHS)[:-1]]
    a_qkv, z_a, qb, kb, vb, z_b, merge_logits, lr_f, lr_b = jnp.split(proj, splits, axis=-1)

    outs, lses = [], []
    a_qkv = a_qkv.reshape(Bsz, S, len(A_GROUPS), 3, A_HEADS, A_HEAD_DIM)
    for gi, (window, dilation) in enumerate(A_GROUPS):
        qg = rope_partial(a_qkv[:, :, gi, 0], pos)
        kg = rope_partial(a_qkv[:, :, gi, 1], pos)
        vg = a_qkv[:, :, gi, 2]
        o_g, lse_g = dilated_window_attention(qg, kg, vg, dilation, window // (2 * dilation))
        outs.append(o_g)
        lses.append(lse_g)
    w_groups = jax.nn.softmax(jnp.stack(lses, axis=0), axis=0)
    o_a = jnp.sum(w_groups[..., None] * jnp.stack(outs, axis=0).astype(jnp.float32), axis=0)
    y_a = o_a.reshape(Bsz, S, A_WIDTH).astype(x.dtype) * jax.nn.silu(z_a)

    def heads(t, d):
        return t.reshape(Bsz, S, B_HEADS, d).transpose(0, 2, 1, 3).astype(jnp.float32)
    q_gla = heads(qb, B_KEY_DIM) * (B_KEY_DIM ** -0.5)
    k_gla = heads(kb, B_KEY_DIM)
    v_gla = heads(vb, B_VAL_DIM)
    g_f = gla_gate(lr_f, gate_up_fwd, gate_bias_fwd)
    g_b = gla_gate(lr_b, gate_up_bwd, gate_bias_bwd)
    o_f = gla_chunked(q_gla, k_gla, v_gla, g_f, strict=False)
    flip = lambda t: jnp.flip(t, axis=2)
    o_bw = flip(gla_chunked(flip(q_gla), flip(k_gla), flip(v_gla), flip(g_b), strict=True))
    o_b = o_f + o_bw
    o_b = o_b * lax.rsqrt(jnp.mean(o_b * o_b, axis=-1, keepdims=True) + NORM_EPS) * gla_out_norm.astype(jnp.float32)
    y_b = o_b.transpose(0, 2, 1, 3).reshape(Bsz, S, B_VAL_WIDTH).astype(x.dtype) * jax.nn.silu(z_b)

    gate_a, gate_b = jnp.split(jax.nn.sigmoid(merge_logits), N_BRANCHES, axis=-1)
    merged = (gate_a * jnp.einsum('bse,ed->bsd', y_a, w_branch_a)
              + gate_b * jnp.einsum('bse,ed->bsd', y_b, w_branch_b))
    out = jnp.einsum('bsd,de->bse', merged, w_out)
    return x + rmsnorm(out, norm_post)


def setup_inputs(seed: int = 0) -> dict:
    key = jax.random.key(seed)
    ks = jax.random.split(key, 13)
    f32 = jnp.float32
    nrm = lambda k, shape, scale: jax.random.normal(k, shape, f32) * scale
    return {
        "x": nrm(ks[0], (BATCH, SEQ, D_MODEL), 1.0),
        "norm_pre": 1.0 + nrm(ks[1], (DEPTH, D_MODEL), 0.02),
        "w_in": nrm(ks[2], (DEPTH, D_MODEL, IN_DIM), D_MODEL ** -0.5),
        "gate_up_fwd": nrm(ks[3], (DEPTH, GATE_RANK, B_KEY_WIDTH), GATE_RANK ** -0.5),
        "gate_bias_fwd": nrm(ks[4], (DEPTH, B_KEY_WIDTH), 0.02),
        "gate_up_bwd": nrm(ks[5], (DEPTH, GATE_RANK, B_KEY_WIDTH), GATE_RANK ** -0.5),
        "gate_bias_bwd": nrm(ks[6], (DEPTH, B_KEY_WIDTH), 0.02),
        "gla_out_norm": 1.0 + nrm(ks[7], (DEPTH, B_VAL_DIM), 0.02),
        "w_branch_a": nrm(ks[8], (DEPTH, A_WIDTH, D_MODEL), A_WIDTH ** -0.5),
        "w_branch_b": nrm(ks[9], (DEPTH, B_VAL_WIDTH, D_MODEL), B_VAL_WIDTH ** -0.5),
        "w_out": nrm(ks[10], (DEPTH, D_MODEL, D_MODEL), D_MODEL ** -0.5),
        "norm_post": 1.0 + nrm(ks[11], (DEPTH, D_MODEL), 0.02),
    }


def reference(x, norm_pre, w_in, gate_up_fwd, gate_bias_fwd, gate_up_bwd, gate_bias_bwd,
              gla_out_norm, w_branch_a, w_branch_b, w_out, norm_post):
    pos = jnp.arange(x.shape[1], dtype=jnp.int32)
    for l in range(DEPTH):
        x = hybrid_layer(x, pos, norm_pre[l], w_in[l], gate_up_fwd[l], gate_bias_fwd[l],
                         gate_up_bwd[l], gate_bias_bwd[l], gla_out_norm[l], w_branch_a[l],
                         w_branch_b[l], w_out[l], norm_post[l])
    return x
```

```python
import numpy as np
import ml_dtypes
import concourse.bass as bass
import concourse.mybir as mybir
from concourse.bass_utils import run_bass_kernel_spmd
from contextlib import ExitStack

F32 = mybir.dt.float32
BF16 = mybir.dt.bfloat16
AF = mybir.ActivationFunctionType
ALU = mybir.AluOpType

S = 4096
D = 1024
NL = 2
NT = 32
EPS = 1e-6
GROUPS = (1, 4, 16)
DEBUG = False

CT = {}


def _build_ct():
    cols = []

    def add(name, c):
        c = list(c) + [-1] * (128 - len(c))
        CT[name] = len(cols)
        cols.append(c)

    perm = list(range(32, 128)) + list(range(0, 32))
    for g in range(3):
        for qk in range(2):
            for h in range(4):
                base = g * 1536 + qk * 512 + h * 128
                add(("a", g, qk, h), [base + p for p in perm])
        for h in range(4):
            base = g * 1536 + 1024 + h * 128
            add(("av", g, h), range(base, base + 128))
    for g in range(3):
        for qk in range(2):
            c = []
            c2 = []
            for h in range(4):
                base = g * 1536 + qk * 512 + h * 128
                c += [base + 16 + i for i in range(16)] + [base + i for i in range(16)]
                c2 += [base + i for i in range(32)]
            add(("sw", g, qk), c)
            add(("ar", g, qk), c2)
    for h in range(4):
        add(("za", h), range(4608 + h * 128, 4608 + (h + 1) * 128))
        add(("qb", h), range(5120 + h * 128, 5120 + (h + 1) * 128))
        add(("kb", h), range(5632 + h * 128, 5632 + (h + 1) * 128))
        for hf in range(2):
            b = 6144 + h * 256 + hf * 128
            add(("vb", h, hf), range(b, b + 128))
            b = 7168 + h * 256 + hf * 128
            add(("zb", h, hf), range(b, b + 128))
    for c8 in range(16):
        add(("gate", c8), range(8192 + c8 * 128, 8192 + (c8 + 1) * 128))
    lr = [-1] * 128
    for i in range(16):
        lr[i] = 10240 + i
        lr[32 + i] = 10256 + i
    add(("lr",), lr)
    return np.array(cols, dtype=np.int64)


CT_COLS = _build_ct()
NCT = CT_COLS.shape[0]

CB_ID, CB_ONES, CB_M1, CB_M2, CB_M3, CB_TF, CB_TB, CB_SU, CB_SL, CB_TFN, CB_TBN = [i * 128 for i in range(11)]
NCB = 11 * 128


def _consts():
    i = np.arange(128)[:, None]
    j = np.arange(128)[None, :]
    cb = np.zeros((128, NCB), np.float32)
    cb[:, CB_ID:CB_ID + 128] = (i == j)
    cb[:, CB_ONES:CB_ONES + 128] = 1.0
    cb[:, CB_M1:CB_M1 + 128] = (i >= j)
    cb[:, CB_M2:CB_M2 + 128] = (i <= j)
    cb[:, CB_M3:CB_M3 + 128] = (i > j)
    cb[:, CB_TF:CB_TF + 128] = (i <= j) * (-1.0 / 16)
    cb[:, CB_TB:CB_TB + 128] = (i >= j) * (-1.0 / 16)
    cb[:, CB_SU:CB_SU + 128] = (i > j) * (-1.0 / 16)
    cb[:, CB_SL:CB_SL + 128] = (i < j) * (-1.0 / 16)
    cb[:, CB_TFN:CB_TFN + 128] = (i <= j) * (1.0 / 16)
    cb[:, CB_TBN:CB_TBN + 128] = (i >= j) * (1.0 / 16)
    inv_freq = (np.float32(500000.0) ** (-np.arange(0, 32, 2, dtype=np.float32) / np.float32(32))).astype(np.float32)
    ang = np.arange(S, dtype=np.float32)[None, :] * inv_freq[:, None]
    cos = np.cos(ang.astype(np.float64))
    sin = np.sin(ang.astype(np.float64))
    ctab = np.concatenate([cos, cos] * 4, axis=0)
    s32 = np.concatenate([-sin, sin], axis=0)
    stab = np.concatenate([s32] * 4, axis=0)
    bf = ml_dtypes.bfloat16
    return cb.astype(bf), ctab.astype(np.float32).astype(bf), stab.astype(np.float32).astype(bf)


class Buf:
    __slots__ = ("name", "w", "r", "excl")

    def __init__(self, name, excl=False):
        self.name = name
        self.w = None
        self.r = {}
        self.excl = excl


class Prog:
    ENG = ("pe", "act", "dve", "pool", "sp")

    def __init__(self, nc, es):
        self.nc = nc
        self.es = es
        self.ops = {e: [] for e in self.ENG}
        self.cnt = {e: 0 for e in self.ENG}
        self.sem = {e: es.enter_context(nc.semaphore("s_" + e)) for e in self.ENG}
        self.seen = {e: {} for e in self.ENG}
        self.dsems = []

    def dma_sem(self, name=None):
        s = self.es.enter_context(self.nc.semaphore(name or ("dq%d" % len(self.dsems))))
        d = [s, 0]
        self.dsems.append(d)
        return d

    def _waits(self, e, reads, writes):
        evs = []
        own = self.sem[e]
        same_ok = e in ("act", "dve", "pool")
        for b in reads:
            if b.w is not None and (b.w[0] is not own or same_ok):
                evs.append(b.w)
            if b.excl:
                for ev in b.r.values():
                    if ev[0] is not own:
                        evs.append(ev)
        for b in writes:
            if b.w is not None and (b.w[0] is not own or (same_ok and FULL_SYNC)):
                evs.append(b.w)
            for ev in b.r.values():
                if ev[0] is not own or (same_ok and FULL_SYNC):
                    evs.append(ev)
        waits = {}
        seen = self.seen[e]
        for (sem, val) in evs:
            k = id(sem)
            if seen.get(k, 0) >= val:
                continue
            if k in waits and waits[k][1] >= val:
                continue
            waits[k] = (sem, val)
        for k, (sem, val) in waits.items():
            seen[k] = val
        return list(waits.values())

    def _record(self, ev, reads, writes):
        k = id(ev[0])
        for b in reads:
            b.r[k] = ev
        for b in writes:
            b.w = ev
            b.r = {}

    def op(self, e, fn, reads=(), writes=()):
        waits = self._waits(e, reads, writes)
        self.cnt[e] += 1
        ev = (self.sem[e], self.cnt[e])
        self.ops[e].append((waits, fn, ev[0], 1))
        self._record(ev, reads, writes)
        return ev

    def dma(self, q, fn, dsem, reads=(), writes=()):
        waits = self._waits(q, reads, writes)
        dsem[1] += 16
        ev = (dsem[0], dsem[1])
        self.ops[q].append((waits, fn, dsem[0], 16))
        self._record(ev, reads, writes)
        return ev

    def barrier(self):
        evs = [(self.sem[e], self.cnt[e]) for e in self.ENG if self.cnt[e] > 0]
        evs += [(d[0], d[1]) for d in self.dsems if d[1] > 0]
        for e in self.ENG:
            waits = []
            for (sem, val) in evs:
                if sem is self.sem[e]:
                    continue
                k = id(sem)
                if self.seen[e].get(k, 0) >= val:
                    continue
                self.seen[e][k] = val
                waits.append((sem, val))
            if waits:
                self.ops[e].append((waits, None, None, 0))

    def emit(self):
        nc = self.nc
        engs = {"pe": "tensor", "act": "scalar", "dve": "vector", "pool": "gpsimd", "sp": "sync"}
        with nc.Block() as block:
            for e in self.ENG:
                ops = self.ops[e]

                def body(eng, ops=ops):
                    for (waits, fn, sem, inc) in ops:
                        for (s, v) in waits:
                            eng.wait_ge(s, v)
                        if fn is not None:
                            fn(eng).then_inc(sem, inc)

                getattr(block, engs[e])(body)


def build(debug=False):
    nc = bass.Bass("TRN2", target_bir_lowering=False)

    def din(name, shape, dt):
        return nc.dram_tensor(name, shape, dt, kind="ExternalInput").ap()

    def dscr(name, shape, dt):
        return nc.dram_tensor(name, shape, dt, kind="Internal").ap()

    x_d = din("x", [S, D], F32)
    wp_d = din("wp", [NL, NCT, 128, 8, 128], F32)
    wa_d = din("wa", [NL, 128, 4, 1024], F32)
    wb_d = din("wb", [NL, 128, 8, 1024], F32)
    wo_d = din("wo", [NL, 128, 8, 1024], F32)
    gup_d = din("gup", [NL, 64, 512], F32)
    npre_d = din("npre", [NL, 1024], F32)
    npost_d = din("npost", [NL, 1024], F32)
    gon_d = din("gon", [NL, 128, 2], F32)
    cb_d = din("cb", [128, NCB], BF16)
    ctab_d = din("ctab", [128, S], BF16)
    stab_d = din("stab", [128, S], BF16)
    y_d = nc.dram_tensor("y", [S, D], F32, kind="ExternalOutput").ap()
    t2_d = dscr("t2s", [6, 128, S], BF16)
    sg_d = dscr("sgs", [16, 128, S], BF16)
    ya_d = dscr("yas", [4, 128, S], BF16)
    yb_d = dscr("ybs", [8, 128, S], BF16)
    szb_d = dscr("szbs", [8, 128, S], BF16)
    x1_d = dscr("x1s", [S, D], F32)
    dbg = {}
    if debug:
        dbg["ya"] = nc.dram_tensor("dbg_ya", [4, 128, S], BF16, kind="ExternalOutput").ap()
        dbg["yb"] = nc.dram_tensor("dbg_yb", [8, 128, S], BF16, kind="ExternalOutput").ap()
        dbg["h"] = nc.dram_tensor("dbg_h", [128, 8, S], BF16, kind="ExternalOutput").ap()
        for nm in ("q", "k", "v"):
            dbg[nm] = nc.dram_tensor("dbg_" + nm, [128, S], BF16, kind="ExternalOutput").ap()
        dbg["oa"] = nc.dram_tensor("dbg_oa", [128, S], F32, kind="ExternalOutput").ap()
        dbg["t2"] = nc.dram_tensor("dbg_t2", [6, 128, S], BF16, kind="ExternalOutput").ap()

    with ExitStack() as es:
        P = Prog(nc, es)

        def sbuf(name, shape, dt):
            return es.enter_context(nc.sbuf_tensor("sb_" + name, shape, dt))

        hT = sbuf("hT", [128, 8, S], BF16)
        NW = 6
        wts = sbuf("wts", [128, NW, 8, 128], BF16)
        cb = sbuf("cb", [128, NCB], BF16)
        AR = 60 * 1024
        arena = sbuf("arena", [128, AR], BF16)
        small = sbuf("small", [128, 256], F32)
        PS = [es.enter_context(nc.psum_tensor("ps%d" % i, [128, 512], F32)) for i in range(8)]
        PSB = [Buf("ps%d" % i, excl=True) for i in range(8)]

        HT = Buf("hT")
        CBB = Buf("cb")
        WB = [Buf("w%d" % i) for i in range(NW)]
        wsem = [P.dma_sem() for _ in range(NW)]
        sem_c = P.dma_sem()
        state = {"wslot": 0, "misc": 0, "aoff": 0, "lay": 0}

        named = {}

        def msem():
            import sys
            f = sys._getframe(1)
            key = (f.f_lineno, named.setdefault(("cnt", f.f_lineno, state["lay"]), [0])[0])
            named[("cnt", f.f_lineno, state["lay"])][0] += 1
            if key not in named:
                named[key] = P.dma_sem()
            return named[key]

        def a_reset():
            state["aoff"] = 0

        def a_bf(n):
            o = state["aoff"]
            assert o + n <= AR, ("arena overflow", o, n)
            state["aoff"] = o + n
            return arena[:, o:o + n]

        def a_f32(n):
            o = state["aoff"]
            o += o % 2
            assert o + 2 * n <= AR, ("arena overflow", o, n)
            state["aoff"] = o + 2 * n
            return arena[:, o:o + 2 * n].bitcast(F32)

        ident = cb[:, CB_ID:CB_ID + 128]
        ones = cb[:, CB_ONES:CB_ONES + 128]

        P.dma("sp", lambda q: q.dma_start(out=cb[:], in_=cb_d), sem_c, writes=[CBB])

        def load_wt(l, ct):
            slot = state["wslot"]
            state["wslot"] = (slot + 1) % NW
            P.dma("pool", lambda q: q.dma_start(out=wts[:, slot], in_=wp_d[l, ct]), wsem[slot], writes=[WB[slot]])
            return slot

        class WStream:
            def __init__(self, l, cts, ahead=3):
                self.l, self.cts, self.ahead = l, cts, ahead
                self.slots = {}
                self.nxt = 0

            def get(self, i):
                while self.nxt < len(self.cts) and self.nxt <= i + self.ahead:
                    self.slots[self.nxt] = load_wt(self.l, self.cts[self.nxt])
                    self.nxt += 1
                return self.slots[i]

        pjr = {"i": 0}

        def next_pj(banks):
            pjr["i"] = (pjr["i"] + 1) % len(banks)
            return banks[pjr["i"]]

        def rhs_class(d, kc, blk):
            if d == 1:
                return hT[:, kc, blk * 512:(blk + 1) * 512]
            v = hT[:, kc, :].rearrange("p (t r) -> p r t", r=d)
            Lc = S // d
            if Lc >= 512:
                r = (blk * 512) // Lc
                t0 = (blk * 512) % Lc
                return v[:, r, t0:t0 + 512]
            nr = 512 // Lc
            return v[:, blk * nr:(blk + 1) * nr, :]

        def tab_class(tab, p0, p1, d, blk):
            if d == 1:
                return tab[p0:p1, blk * 512:(blk + 1) * 512]
            v = tab[p0:p1, :].rearrange("p (t r) -> p r t", r=d)
            Lc = S // d
            if Lc >= 512:
                r = (blk * 512) // Lc
                t0 = (blk * 512) % Lc
                return v[:, r, t0:t0 + 512]
            nr = 512 // Lc
            return v[:, blk * nr:(blk + 1) * nr, :]

        def shp(ap, d):
            Lc = S // d
            if d == 1 or Lc >= 512:
                return ap
            return ap.rearrange("p (r t) -> p r t", t=Lc)

        def proj_fm(slot, rhs_fn, evac_fn, banks, nblk=8, d=1):
            for blk in range(nblk):
                pi = next_pj(banks)
                for kc in range(8):
                    P.op("pe", lambda e, pi=pi, kc=kc, blk=blk: e.matmul(
                        shp(PS[pi][:, 0:512], d), lhsT=wts[:, slot, kc, :], rhs=rhs_fn(kc, blk),
                        start=(kc == 0), stop=(kc == 7)), reads=[WB[slot], HT], writes=[PSB[pi]])
                evac_fn(blk, pi)

        cpr = {"i": 0}

        DEF_ENGS = ("dve", "act", "dve", "act", "pool", "dve", "act", "pool")

        def stage_blk(d, ring, RINGB, blk, engs=DEF_ENGS):
            if d == 1:
                return
            nr = len(ring)
            for kc in range(8):
                ri = (blk * 8 + kc) % nr
                eng = engs[kc]
                if eng == "act":
                    P.op("act", lambda e, ri=ri, kc=kc: e.activation(out=shp(ring[ri], d), in_=rhs_class(d, kc, blk), func=AF.Copy), reads=[HT], writes=[RINGB[ri]])
                else:
                    P.op(eng, lambda e, ri=ri, kc=kc: e.tensor_copy(out=shp(ring[ri], d), in_=rhs_class(d, kc, blk)), reads=[HT], writes=[RINGB[ri]])

        def proj_multi(slots, evac_fns, banksets, d, ring, RINGB, engs=DEF_ENGS, prestaged=False):
            nr = len(ring)

            def stage(blk):
                stage_blk(d, ring, RINGB, blk, engs)

            if not prestaged:
                stage(0)
            for blk in range(8):
                bs = banksets[blk % len(banksets)]
                for kc in range(8):
                    ri = (blk * 8 + kc) % nr
                    for t, slot in enumerate(slots):
                        if d == 1:
                            P.op("pe", lambda e, t=t, slot=slot, kc=kc, bs=bs, blk=blk: e.matmul(
                                PS[bs[t]][:, 0:512], lhsT=wts[:, slot, kc, :], rhs=rhs_class(1, kc, blk), start=(kc == 0), stop=(kc == 7)),
                                reads=[WB[slot], HT], writes=[PSB[bs[t]]])
                        else:
                            P.op("pe", lambda e, t=t, slot=slot, kc=kc, ri=ri, bs=bs: e.matmul(
                                PS[bs[t]][:, 0:512], lhsT=wts[:, slot, kc, :], rhs=ring[ri], start=(kc == 0), stop=(kc == 7)),
                                reads=[WB[slot], RINGB[ri]], writes=[PSB[bs[t]]])
                if blk < 7:
                    stage(blk + 1)
                for t, evf in enumerate(evac_fns):
                    evf(blk, bs[t])

        def layer(l, x_in, x_out, XIN, XOUT):
            state["lay"] = l
            a_reset()
            wpre = a_f32(1024)
            xts = [a_f32(1024) for _ in range(4)]
            hns = [a_bf(1024) for _ in range(2)]
            junk = a_bf(1024)
            XT = [Buf("xt%d" % i) for i in range(4)]
            HN = [Buf("hn0"), Buf("hn1")]
            JK = Buf("junk")
            WPRE = Buf("wpre")
            SMs = [Buf("small0"), Buf("small1")]
            xsem = [msem(), msem(), msem(), msem()]
            P.op("pool", lambda e: e.memset(small[:, 0:64], 0.0), writes=SMs)
            P.dma("sp", lambda q: q.dma_start(out=wpre, in_=npre_d[l:l + 1, :].partition_broadcast(128)), msem(), writes=[WPRE])
            def ldx(i):
                bx = i % 4
                P.dma("sp", lambda q: q.dma_start(out=xts[bx], in_=x_in[i * 128:(i + 1) * 128, :]), xsem[bx], reads=[XIN], writes=[XT[bx]])

            def evac_hT(i):
                pi = 6 + (i % 2)
                psb = PS[pi][:, 0:512].bitcast(BF16)
                P.op("dve", lambda e: e.tensor_copy(out=hT[:, :, i * 128:(i + 1) * 128], in_=psb.rearrange("p (k c) -> p k c", c=128)),
                     reads=[PSB[pi]], writes=[Buf("hTt")])

            for i in range(3):
                ldx(i)
            for i in range(NT):
                b = i % 2
                bx = i % 4
                if i + 3 < NT:
                    ldx(i + 3)
                P.op("act", lambda e, i=i, bx=bx: e.activation(out=junk, in_=xts[bx], func=AF.Square, accum_out=small[:, i:i + 1]),
                     reads=[XT[bx]], writes=[JK, SMs[b]])
                P.op("act", lambda e, i=i: e.activation(out=small[:, 64 + i:65 + i], in_=small[:, i:i + 1], func=AF.Ln, scale=1.0 / D, bias=EPS),
                     reads=[SMs[b]], writes=[SMs[b]])
                P.op("act", lambda e, i=i: e.activation(out=small[:, 128 + i:129 + i], in_=small[:, 64 + i:65 + i], func=AF.Exp, scale=-0.5),
                     reads=[SMs[b]], writes=[SMs[b]])
                P.op("dve", lambda e, i=i, b=b, bx=bx: e.scalar_tensor_tensor(out=hns[b], in0=xts[bx], scalar=small[:, 128 + i:129 + i], in1=wpre,
                                                                      op0=ALU.mult, op1=ALU.mult),
                     reads=[XT[bx], SMs[b], WPRE], writes=[HN[b]])
                pi = 6 + (i % 2)
                psb = PS[pi][:, 0:512].bitcast(BF16)
                for k in range(8):
                    P.op("pe", lambda e, k=k, b=b, psb=psb: e.transpose(psb[:, k * 128:(k + 1) * 128], hns[b][:, k * 128:(k + 1) * 128], ident),
                         reads=[HN[b], CBB], writes=[PSB[pi]])
                if i >= 1:
                    evac_hT(i - 1)
            evac_hT(NT - 1)
            P.barrier()
            if STOP == "A":
                return [], []

            a_reset()
            a_b0 = 0
            stab = a_bf(S)
            ctab = a_bf(S)
            STAB = Buf("stab")
            stg = [a_bf(S) for _ in range(2)]
            STG = [Buf("stg0"), Buf("stg1")]
            ring = [a_bf(512) for _ in range(8)]
            RINGB = [Buf("ring%d" % i) for i in range(8)]
            tmpA = [a_f32(S) for _ in range(2)]
            TMPA = [[Buf("tmpA%d_%d" % (i, j)) for j in range(8)] for i in range(2)]
            t2f = [a_f32(512) for _ in range(2)]
            T2F = [Buf("t2f0"), Buf("t2f1")]
            stgsem = [msem(), msem()]
            P.dma("sp", lambda q: q.dma_start(out=stab, in_=stab_d), msem(), writes=[STAB])
            CTB0 = Buf("ctab0")
            P.dma("sp", lambda q: q.dma_start(out=ctab, in_=ctab_d), msem(), writes=[CTB0])
            cts = []
            for g in range(3):
                cts += [CT[("ar", g, 0)], CT[("sw", g, 0)], CT[("ar", g, 1)], CT[("sw", g, 1)]]
            cts += [CT[("gate", c)] for c in range(16)]
            cts += [CT[("zb", h, hf)] for h in range(4) for hf in range(2)]
            ws = WStream(l, cts, ahead=1)
            SZD = [Buf("szd%d" % j) for j in range(8)]
            T2D = [Buf("t2d%d" % j) for j in range(6)]
            SGD = [Buf("sgd%d" % j) for j in range(16)]
            banks = [0, 1, 2, 3]
            n = 0
            rt = {"i": 0}
            for g in range(3):
                d = GROUPS[g]

                def mk_ar(d, qk):
                    def ev(blk, pi):
                        P.op("dve", lambda e: e.tensor_tensor(out=shp(tmpA[qk][:, blk * 512:(blk + 1) * 512], d), in0=shp(PS[pi][:, 0:512], d),
                                                             in1=tab_class(ctab, 0, 128, d, blk), op=ALU.mult),
                             reads=[PSB[pi], CTB0], writes=[TMPA[qk][blk]])
                    return ev

                def mk_sw(d, qk):
                    def ev(blk, pi):
                        tb = rt["i"] = (rt["i"] + 1) % 2
                        P.op("dve", lambda e: e.tensor_tensor(out=shp(t2f[tb], d), in0=shp(PS[pi][:, 0:512], d),
                                                             in1=tab_class(stab, 0, 128, d, blk), op=ALU.mult),
                             reads=[PSB[pi], STAB], writes=[T2F[tb]])
                        P.op("pool", lambda e: e.tensor_tensor(out=stg[qk][:, blk * 512:(blk + 1) * 512], in0=t2f[tb], in1=tmpA[qk][:, blk * 512:(blk + 1) * 512], op=ALU.add),
                             reads=[T2F[tb], TMPA[qk][blk]], writes=[STG[qk]])
                    return ev

                slots = [ws.get(n + i) for i in range(4)]
                n += 4
                proj_multi(slots, [mk_ar(d, 0), mk_sw(d, 0), mk_ar(d, 1), mk_sw(d, 1)], [[0, 1, 2, 3], [4, 5, 6, 7]], d, ring, RINGB,
                           engs=("dve", "act", "act", "dve", "act", "act", "dve", "act"))
                for qk in range(2):
                    j = g * 2 + qk
                    P.dma("sp", lambda q, j=j, qk=qk: q.dma_start(out=t2_d[j], in_=stg[qk]), stgsem[qk], reads=[STG[qk]], writes=[T2D[j]])
            for c in range(16):
                sb = n % 2
                slot = ws.get(n)

                def ev(blk, pi, sb=sb):
                    P.op("act", lambda e: e.activation(out=stg[sb][:, blk * 512:(blk + 1) * 512], in_=PS[pi][:, 0:512], func=AF.Sigmoid),
                         reads=[PSB[pi]], writes=[STG[sb]])

                proj_fm(slot, lambda kc, blk: rhs_class(1, kc, blk), ev, banks)
                P.dma("sp", lambda q, c=c, sb=sb: q.dma_start(out=sg_d[c], in_=stg[sb]), stgsem[sb], reads=[STG[sb]], writes=[SGD[c]])
                n += 1
            for c in range(8):
                sb = n % 2
                slot = ws.get(n)

                def ev(blk, pi, sb=sb):
                    P.op("act", lambda e: e.activation(out=stg[sb][:, blk * 512:(blk + 1) * 512], in_=PS[pi][:, 0:512], func=AF.Silu),
                         reads=[PSB[pi]], writes=[STG[sb]])

                proj_fm(slot, lambda kc, blk: rhs_class(1, kc, blk), ev, banks)
                P.dma("sp", lambda q, c=c, sb=sb: q.dma_start(out=szb_d[c], in_=stg[sb]), stgsem[sb], reads=[STG[sb]], writes=[SZD[c]])
                n += 1
            P.barrier()
            if STOP == "B0":
                return [], []

            state["aoff"] = a_b0
            qT = a_bf(S)
            kT = a_bf(S)
            vA = a_bf(S)
            vA3 = vA.rearrange("p (t c) -> p t c", c=128)
            acc = a_f32(2 * S)
            accU = acc[:, 0:S]
            accD = acc[:, S:2 * S]
            yast = a_bf(S)
            Es = [a_bf(256) for _ in range(4)]
            Pm = [a_bf(256) for _ in range(4)]
            szs = [a_f32(512) for _ in range(2)]
            vtb = [a_bf(512) for _ in range(2)]
            VTB = [Buf("vtb0"), Buf("vtb1")]
            ring = [a_bf(512) for _ in range(8)]
            RINGB = [Buf("ringb%d" % i) for i in range(8)]
            YAST = Buf("yast")
            QTR, KTR = Buf("qTR"), Buf("kTR")
            QTa = [Buf("qTa%d" % i) for i in range(8)]
            KTa = [Buf("kTa%d" % i) for i in range(8)]
            QT = QTa + [QTR]
            KT = KTa + [KTR]
            VA = [Buf("vA%d" % i) for i in range(8)]
            ACC = [Buf("acc%d" % i) for i in range(3)]
            ACCF = Buf("accF")
            EB = [Buf("E%d" % i) for i in range(4)]
            PMB = [Buf("Pm%d" % i) for i in range(4)]
            SZ = [Buf("sz0"), Buf("sz1")]
            t2sem = [msem(), msem()]
            yasem = msem()
            YAD = [Buf("yad%d" % h) for h in range(4)]
            cts = []
            for h in range(4):
                for g in range(3):
                    cts += [CT[("a", g, 0, h)], CT[("a", g, 1, h)], CT[("av", g, h)]]
                cts += [CT[("za", h)]]
            ws = WStream(l, cts)
            n = 0
            pjb = [0, 1, 2]
            rr = {"t": 0, "e": 0}
            for h in range(4):
                for g in range(3):
                    d = GROUPS[g]
                    Lc = S // d
                    evs = []
                    for qk, (dst, DSTa, DSTR) in enumerate(((qT, QTa, QTR), (kT, KTa, KTR))):
                        j = g * 2 + qk
                        P.dma("sp", lambda q, j=j, h=h, dst=dst: q.dma_start(out=dst[96:128, :], in_=t2_d[j, 32 * h:32 * h + 32, :]), t2sem[qk],
                              reads=[T2D[j]], writes=[DSTR])

                        def ev(blk, pi, dst=dst, DSTa=DSTa):
                            P.op("act", lambda e: e.activation(out=dst[0:96, blk * 512:(blk + 1) * 512], in_=PS[pi][0:96, 0:512], func=AF.Copy),
                                 reads=[PSB[pi]], writes=[DSTa[blk]])

                        evs.append(ev)

                    pend = []

                    def v_tr(blk):
                        vb_ = blk % 2
                        tbk = 3 + (blk % 2)
                        psb7 = PS[tbk][:, 0:256].bitcast(BF16)
                        for tt in range(4):
                            P.op("pe", lambda e, tt=tt: e.transpose(psb7[:, tt * 128:(tt + 1) * 128], vtb[vb_][:, tt * 128:(tt + 1) * 128], ident),
                                 reads=[VTB[vb_], CBB], writes=[PSB[tbk]])
                        P.op("dve", lambda e: e.tensor_copy(out=vA[:, blk * 512:(blk + 1) * 512], in_=psb7), reads=[PSB[tbk]], writes=[VA[blk]])

                    def ev_v(blk, pi):
                        vb_ = blk % 2
                        P.op("act", lambda e: e.activation(out=vtb[vb_], in_=PS[pi][:, 0:512], func=AF.Copy), reads=[PSB[pi]], writes=[VTB[vb_]])
                        if pend:
                            v_tr(pend.pop())
                        pend.append(blk)

                    evs.append(ev_v)
                    slots = [ws.get(n), ws.get(n + 1), ws.get(n + 2)]
                    n += 3
                    if d == 1:
                        for t in range(3):
                            proj_fm(slots[t], lambda kc, blk: rhs_class(1, kc, blk), evs[t], pjb)
                    else:
                        proj_multi(slots, evs, [[0, 1, 2], [5, 6, 7]], d, ring, RINGB, prestaged=True)
                    v_tr(pend.pop())
                    tiles = []
                    ntc = Lc // 128
                    for r in range(d):
                        for i in range(ntc + 1):
                            tq0 = max(0, 128 * i - 64)
                            tq1 = min(Lc, 128 * i + 64)
                            kts = []
                            if i >= 1:
                                kts.append((r * ntc + i - 1, CB_M1))
                            if i < ntc:
                                kts.append((r * ntc + i, CB_M2))
                            b0 = tq0 - (128 * i - 64)
                            tiles.append((r, tq0, tq1 - tq0, b0, kts))

                    def front(ti):
                        r, tq0, nq, b0, kts = tiles[ti]
                        sb = (3, 4, 0)[ti % 3]
                        ei = ti % 4
                        mq0 = r * Lc + tq0
                        for ki, (kt, mcol) in enumerate(kts):
                            P.op("pe", lambda e, ki=ki, kt=kt: e.matmul(PS[sb][:, ki * 128:ki * 128 + nq], lhsT=kT[:, kt * 128:(kt + 1) * 128],
                                                                       rhs=qT[:, mq0:mq0 + nq], start=True, stop=True),
                                 reads=KT + QT, writes=[PSB[sb]])
                        nk = len(kts)
                        if nk == 2:
                            P.op("act", lambda e: e.activation(out=Es[ei][:, 0:256], in_=PS[sb][:, 0:256], func=AF.Exp, scale=128 ** -0.5),
                                 reads=[PSB[sb]], writes=[EB[ei]])
                            P.op("dve" if ti % 2 == 0 else "pool", lambda e: e.tensor_tensor(out=Pm[ei][:, 0:256], in0=Es[ei][:, 0:256], in1=cb[:, CB_M1:CB_M1 + 256], op=ALU.mult),
                                 reads=[EB[ei], CBB], writes=[PMB[ei]])
                        else:
                            mcol = kts[0][1]
                            P.op("act", lambda e: e.activation(out=Es[ei][:, 0:nq], in_=PS[sb][:, 0:nq], func=AF.Exp, scale=128 ** -0.5),
                                 reads=[PSB[sb]], writes=[EB[ei]])
                            P.op("dve" if ti % 2 == 0 else "pool", lambda e: e.tensor_tensor(out=Pm[ei][:, 0:nq], in0=Es[ei][:, 0:nq], in1=cb[:, mcol + b0:mcol + b0 + nq], op=ALU.mult),
                                 reads=[EB[ei], CBB], writes=[PMB[ei]])

                    def back(ti):
                        r, tq0, nq, b0, kts = tiles[ti]
                        ub = (5, 6, 1)[ti % 3]
                        ei = ti % 4
                        nk = len(kts)
                        for ki, (kt, mcol) in enumerate(kts):
                            P.op("pe", lambda e, ki=ki, kt=kt: e.matmul(PS[ub][:, 0:nq], lhsT=vA3[:, kt, :], rhs=Pm[ei][:, ki * 128:ki * 128 + nq],
                                                                       start=(ki == 0), stop=(ki == nk - 1)),
                                 reads=VA + [PMB[ei]], writes=[PSB[ub]])
                        for ki, (kt, mcol) in enumerate(kts):
                            P.op("pe", lambda e, ki=ki: e.matmul(PS[ub][:, 128:128 + nq], lhsT=ones, rhs=Pm[ei][:, ki * 128:ki * 128 + nq],
                                                                start=(ki == 0), stop=(ki == nk - 1)),
                                 reads=[CBB, PMB[ei]], writes=[PSB[ub]])
                        if d == 1:
                            oA = acc.rearrange("p (a t) -> p a t", a=2)[:, :, tq0:tq0 + nq]
                        else:
                            oA = acc.rearrange("p (a t r) -> p a r t", a=2, r=d)[:, :, r, tq0:tq0 + nq]
                        pA = PS[ub][:, 0:256].rearrange("p (a t) -> p a t", a=2)[:, :, 0:nq]
                        if g == 0:
                            P.op("act", lambda e: e.activation(out=oA, in_=pA, func=AF.Copy), reads=[PSB[ub]], writes=[ACC[0], ACCF])
                        else:
                            P.op("dve", lambda e: e.tensor_tensor(out=oA, in0=pA, in1=oA, op=ALU.add), reads=[PSB[ub], ACC[g - 1]], writes=[ACC[g]])

                    if g < 2:
                        stage_blk(GROUPS[g + 1], ring, RINGB, 0)
                    nti = len(tiles)
                    for ti in range(nti + 3):
                        if ti < nti:
                            front(ti)
                        if ti >= 3:
                            back(ti - 3)
                slot = ws.get(n)
                n += 1

                for blk in range(8):
                    sl = slice(blk * 512, (blk + 1) * 512)
                    P.op("act", lambda e, sl=sl: e.activation(out=accD[:, sl], in_=accD[:, sl], func=AF.Ln), reads=[ACC[2]], writes=[ACCF])
                    P.op("act", lambda e, sl=sl: e.activation(out=accD[:, sl], in_=accD[:, sl], func=AF.Exp, scale=-1.0), reads=[ACCF], writes=[ACCF])

                def ev_z(blk, pi):
                    zb = blk % 2
                    sl = slice(blk * 512, (blk + 1) * 512)
                    P.op("act", lambda e: e.activation(out=szs[zb], in_=PS[pi][:, 0:512], func=AF.Silu), reads=[PSB[pi]], writes=[SZ[zb]])
                    P.op("dve", lambda e: e.tensor_tensor(out=accU[:, sl], in0=accU[:, sl], in1=accD[:, sl], op=ALU.mult),
                         reads=[ACC[2], ACCF], writes=[ACCF])
                    P.op("pool", lambda e: e.tensor_tensor(out=yast[:, sl], in0=accU[:, sl], in1=szs[zb], op=ALU.mult),
                         reads=[ACCF, SZ[zb]], writes=[YAST])

                proj_fm(slot, lambda kc, blk: rhs_class(1, kc, blk), ev_z, pjb)
                P.dma("sp", lambda q, h=h: q.dma_start(out=ya_d[h], in_=yast), yasem, reads=[YAST], writes=[YAD[h]])
            P.barrier()
            if debug and STOP == "B":
                dq = P.dma_sem()
                for nm, t in (("q", qT), ("k", kT), ("v", vA), ("oa", accU)):
                    P.dma("sp", lambda q, nm=nm, t=t: q.dma_start(out=dbg[nm], in_=t), dq, writes=[Buf("x")])
                P.barrier()
            if STOP == "B":
                return YAD, []

            state["aoff"] = a_b0
            lrT = a_bf(S)
            LRT = Buf("lrT")
            P.op("pool", lambda e: e.memset(lrT[0:64, :], 1.0), writes=[LRT])
            gup = a_bf(512)
            GUP = Buf("gup")
            gon = small[:, 200:202]
            GON = Buf("gon")
            P.dma("pool", lambda q: q.dma_start(out=gup[0:64, :], in_=gup_d[l]), msem(), writes=[GUP])
            P.dma("sp", lambda q: q.dma_start(out=gon, in_=gon_d[l]), msem(), writes=[GON])
            qbT = a_bf(S)
            kbT = a_bf(S)
            vB = a_bf(32 * 256)
            vB3 = vB.rearrange("p (t c) -> p t c", c=256)
            lg = a_bf(32 * 256)
            lg3 = lg.rearrange("p (t c) -> p t c", c=256)
            qdb = a_bf(S)
            kib = a_bf(S)
            SbA = a_bf(32 * 256)
            Sb3 = SbA.rearrange("p (t c) -> p t c", c=256)
            Sst2 = [a_f32(256) for _ in range(2)]
            Sring = [a_bf(256) for _ in range(3)]
            szb = [a_bf(1024) for _ in range(2)]
            ybst = [a_bf(1024) for _ in range(2)]
            NTMP = 3
            EWt = [a_f32(384) for _ in range(NTMP)]
            Ef = [t[:, 0:128] for t in EWt]
            Efi = [t[:, 128:256] for t in EWt]
            Wd = [t[:, 256:384] for t in EWt]
            qd = [a_bf(128) for _ in range(NTMP)]
            ki_ = [a_bf(128) for _ in range(NTMP)]
            ke = [a_bf(128) for _ in range(NTMP)]
            ABm = [a_bf(256) for _ in range(NTMP)]
            sq = [a_bf(256) for _ in range(2)]
            lnv = [a_f32(128) for _ in range(2)]
            rstd = [a_f32(128) for _ in range(2)]
            tt_ = [a_f32(256) for _ in range(2)]
            e1 = [a_f32(512) for _ in range(2)]
            QB, KB, LG, QDB, KIB, SBB = [Buf(nm) for nm in "qbT kbT lg qdb kib Sb".split()]
            SST2 = [Buf("Sst0"), Buf("Sst1")]
            VB = [Buf("vB%d" % i) for i in range(16)]
            SR = [Buf("Sr%d" % i) for i in range(3)]
            SZB = [Buf("szb0"), Buf("szb1")]
            YBST = [Buf("ybst0"), Buf("ybst1")]
            EF = [Buf("EW%d" % i) for i in range(NTMP)]
            EFI = EF
            WD = EF
            QD = [Buf("qd%d" % i) for i in range(NTMP)]
            KI = [Buf("ki%d" % i) for i in range(NTMP)]
            KE = [Buf("ke%d" % i) for i in range(NTMP)]
            AB = [Buf("AB%d" % i) for i in range(NTMP)]
            SQ = [Buf("sq0"), Buf("sq1")]
            LNV = [Buf("lnv0"), Buf("lnv1")]
            RSTD = [Buf("rstd0"), Buf("rstd1")]
            TT = [[Buf("tt%d%d" % (i, j)) for j in range(2)] for i in range(2)]
            E1 = [Buf("e10"), Buf("e11")]
            ybsem = [[msem(), msem()], [msem(), msem()]]
            YBD = [Buf("ybd%d" % i) for i in range(8)]
            slot_lr = load_wt(l, CT[("lr",)])

            def ev_lr(blk, pi):
                P.op("act", lambda e: e.activation(out=lrT[0:16, blk * 512:(blk + 1) * 512], in_=PS[pi][0:16, 0:512], func=AF.Copy),
                     reads=[PSB[pi]], writes=[LRT])
                P.op("act", lambda e: e.activation(out=lrT[32:48, blk * 512:(blk + 1) * 512], in_=PS[pi][32:48, 0:512], func=AF.Copy),
                     reads=[PSB[pi]], writes=[LRT])

            proj_fm(slot_lr, lambda kc, blk: rhs_class(1, kc, blk), ev_lr, [0, 1])
            cts = []
            for h in range(4):
                cts += [CT[("qb", h)], CT[("kb", h)], CT[("vb", h, 0)], CT[("vb", h, 1)]]
            ws = WStream(l, cts, ahead=2)
            szsem = [msem(), msem()]
            pjb = [0, 1]
            def gla_head(h):
                n = h * 4
                slot = ws.get(n)

                def ev_q(blk, pi):
                    P.op("act", lambda e: e.activation(out=qbT[:, blk * 512:(blk + 1) * 512], in_=PS[pi][:, 0:512], func=AF.Copy, scale=128 ** -0.5),
                         reads=[PSB[pi]], writes=[QB])

                proj_fm(slot, lambda kc, blk: rhs_class(1, kc, blk), ev_q, pjb)
                slot = ws.get(n + 1)

                def ev_k(blk, pi):
                    P.op("dve", lambda e: e.tensor_copy(out=kbT[:, blk * 512:(blk + 1) * 512], in_=PS[pi][:, 0:512]), reads=[PSB[pi]], writes=[KB])

                proj_fm(slot, lambda kc, blk: rhs_class(1, kc, blk), ev_k, pjb)
                def lg_step(tg):
                    eb = tg % 2
                    px = [2, 4][tg % 2]
                    py = [3, 5][tg % 2]
                    for tt in range(4):
                        tj = tg * 4 + tt
                        P.op("pe", lambda e, px=px, tt=tt, tj=tj: e.matmul(PS[px][:, tt * 128:(tt + 1) * 128], lhsT=lrT[0:32, tj * 128:(tj + 1) * 128],
                                                                         rhs=gup[0:32, h * 128:(h + 1) * 128], start=True, stop=True),
                             reads=[LRT, GUP], writes=[PSB[px]])
                        P.op("pe", lambda e, py=py, tt=tt, tj=tj: e.matmul(PS[py][:, tt * 128:(tt + 1) * 128], lhsT=lrT[32:64, tj * 128:(tj + 1) * 128],
                                                                         rhs=gup[32:64, h * 128:(h + 1) * 128], start=True, stop=True),
                             reads=[LRT, GUP], writes=[PSB[py]])
                    for di, pz in enumerate((px, py)):
                        P.op("act", lambda e, pz=pz, di=di: e.activation(out=e1[di], in_=PS[pz][:, 0:512], func=AF.Exp, scale=-1.0),
                             reads=[PSB[pz]], writes=[E1[di]])
                        P.op("act", lambda e, di=di, tg=tg: e.activation(out=lg3[:, tg * 4:(tg + 1) * 4, di * 128:(di + 1) * 128],
                                                                       in_=e1[di].rearrange("p (t c) -> p t c", c=128), func=AF.Ln, bias=1.0),
                             reads=[E1[di]], writes=[LG])
                s0 = ws.get(n + 2)
                s1 = ws.get(n + 3)

                def v_step(tg):
                    pi = next_pj(pjb)
                    for tt in range(2):
                        tj = tg * 2 + tt
                        if s1 == s0 + 1:
                            for kc in range(8):
                                P.op("pe", lambda e, pi=pi, tt=tt, tj=tj, kc=kc: e.matmul(
                                    PS[pi][:, tt * 256:(tt + 1) * 256].rearrange("p (a c) -> p a c", a=2), lhsT=hT[:, kc, tj * 128:(tj + 1) * 128],
                                    rhs=wts[:, s0:s0 + 2, kc, :], start=(kc == 0), stop=(kc == 7)),
                                    reads=[WB[s0], WB[s1], HT], writes=[PSB[pi]])
                        else:
                            for hf, sl in enumerate((s0, s1)):
                                for kc in range(8):
                                    P.op("pe", lambda e, pi=pi, tt=tt, tj=tj, hf=hf, sl=sl, kc=kc: e.matmul(
                                        PS[pi][:, tt * 256 + hf * 128:tt * 256 + hf * 128 + 128], lhsT=hT[:, kc, tj * 128:(tj + 1) * 128],
                                        rhs=wts[:, sl, kc, :], start=(kc == 0), stop=(kc == 7)),
                                        reads=[WB[sl], HT], writes=[PSB[pi]])
                    P.op("dve", lambda e, pi=pi, tg=tg: e.tensor_copy(out=vB[:, tg * 512:(tg + 1) * 512], in_=PS[pi][:, 0:512]),
                         reads=[PSB[pi]], writes=[VB[tg]])

                for i_ in range(16):
                    v_step(i_)
                    if i_ % 2 == 1:
                        lg_step(i_ // 2)
                if STOP == "C0":
                    return
                if STOP == "C1":
                    return
                P.op("pool", lambda e: e.memset(Sst2[0], 0.0), writes=[SST2[0]])
                P.op("pool", lambda e: e.memset(Sb3[:, 31, :], 0.0), writes=[SBB])

                def p1a(cn):
                    ti = cn % NTMP
                    fa = 2 + (cn % 2)
                    fb = 4 + (cn % 2)
                    cs = slice(cn * 128, (cn + 1) * 128)
                    P.op("pe", lambda e: e.matmul(PS[fa][:, 0:128], lhsT=lg3[:, cn, 128:256], rhs=cb[:, CB_TB:CB_TB + 128], start=True, stop=True),
                         reads=[LG, CBB], writes=[PSB[fa]])
                    P.op("pe", lambda e: e.matmul(PS[fa][:, 128:256], lhsT=lg3[:, cn, 128:256], rhs=cb[:, CB_TBN:CB_TBN + 128], start=True, stop=True),
                         reads=[LG, CBB], writes=[PSB[fa]])
                    P.op("pe", lambda e: e.matmul(PS[fa][:, 256:384], lhsT=cb[:, CB_SL:CB_SL + 128], rhs=lg3[:, cn, 128:256], start=True, stop=True),
                         reads=[LG, CBB], writes=[PSB[fa]])
                    kps = PS[fa][:, 384:448].bitcast(BF16)
                    if cn >= 1:
                        P.op("pe", lambda e: e.transpose(kps, kbT[:, cs], ident), reads=[KB, CBB], writes=[PSB[fa]])
                    P.op("act", lambda e: e.activation(out=EWt[ti], in_=PS[fa][:, 0:384], func=AF.Exp), reads=[PSB[fa]], writes=[EF[ti]])
                    P.op("dve", lambda e: e.tensor_tensor(out=qdb[:, cs], in0=qbT[:, cs], in1=Ef[ti], op=ALU.mult), reads=[QB, EF[ti]], writes=[QDB])
                    P.op("pool", lambda e: e.tensor_tensor(out=kib[:, cs], in0=kbT[:, cs], in1=Efi[ti], op=ALU.mult), reads=[KB, EFI[ti]], writes=[KIB])
                    if cn >= 1:
                        P.op("dve", lambda e: e.tensor_tensor(out=ke[ti], in0=kps, in1=Wd[ti], op=ALU.mult), reads=[PSB[fa], WD[ti]], writes=[KE[ti]])

                def p1b(cn):
                    if cn < 1:
                        return
                    ti = cn % NTMP
                    fb = 4 + (cn % 2)
                    src, dst = Sst2[(31 - cn) % 2], Sst2[(32 - cn) % 2]
                    SRC, DST = SST2[(31 - cn) % 2], SST2[(32 - cn) % 2]
                    P.op("pe", lambda e: e.matmul(PS[fb][:, 0:256], lhsT=ke[ti], rhs=vB3[:, cn, :], start=True, stop=True),
                         reads=[KE[ti], VB[cn // 2]], writes=[PSB[fb]])
                    P.op("dve", lambda e: e.scalar_tensor_tensor(out=dst, in0=src, scalar=Ef[ti][:, 0:1], in1=PS[fb][:, 0:256], op0=ALU.mult, op1=ALU.add),
                         reads=[SRC, EF[ti], PSB[fb]], writes=[DST])

                def p1c(cn):
                    if cn < 1:
                        return
                    P.op("act", lambda e: e.activation(out=Sb3[:, cn - 1, :], in_=Sst2[(32 - cn) % 2], func=AF.Copy), reads=[SST2[(32 - cn) % 2]], writes=[SBB])

                for cn in range(31, -3, -1):
                    if cn >= 0:
                        p1a(cn)
                    if 0 <= cn + 2 <= 31:
                        p1c(cn + 2)
                    if 0 <= cn + 1 <= 31:
                        p1b(cn + 1)
                if STOP == "C2":
                    return
                P.op("pool", lambda e: e.memset(Sst2[1], 0.0), writes=[SST2[1]])

                def s1(cn):
                    ti = cn % NTMP
                    fa = 2 + (cn % 2)
                    fb = 4 + (cn % 2)
                    cs = slice(cn * 128, (cn + 1) * 128)
                    if cn % 4 == 0:
                        blk = cn // 4
                        zb = blk % 2
                        P.dma("sp", lambda q: q.dma_start(out=szb[zb].rearrange("p (k t) -> p k t", t=512),
                                                         in_=szb_d[2 * h:2 * h + 2, :, blk * 512:(blk + 1) * 512].rearrange("k p t -> p k t")),
                              szsem[zb], reads=[SZD[2 * h], SZD[2 * h + 1]], writes=[SZB[zb]])
                    P.op("pe", lambda e: e.matmul(PS[fa][:, 0:128], lhsT=lg3[:, cn, 0:128], rhs=cb[:, CB_TF:CB_TF + 128], start=True, stop=True),
                         reads=[LG, CBB], writes=[PSB[fa]])
                    P.op("pe", lambda e: e.matmul(PS[fa][:, 128:256], lhsT=lg3[:, cn, 0:128], rhs=cb[:, CB_TFN:CB_TFN + 128], start=True, stop=True),
                         reads=[LG, CBB], writes=[PSB[fa]])
                    P.op("pe", lambda e: e.matmul(PS[fa][:, 256:384], lhsT=cb[:, CB_SU:CB_SU + 128], rhs=lg3[:, cn, 0:128], start=True, stop=True),
                         reads=[LG, CBB], writes=[PSB[fa]])
                    kps = PS[fa][:, 384:448].bitcast(BF16)
                    P.op("pe", lambda e: e.transpose(kps, kbT[:, cs], ident), reads=[KB, CBB], writes=[PSB[fa]])
                    P.op("act", lambda e: e.activation(out=EWt[ti], in_=PS[fa][:, 0:384], func=AF.Exp), reads=[PSB[fa]], writes=[EF[ti]])
                    P.op("dve", lambda e: e.tensor_tensor(out=qd[ti], in0=qbT[:, cs], in1=Ef[ti], op=ALU.mult), reads=[QB, EF[ti]], writes=[QD[ti]])
                    P.op("pool", lambda e: e.tensor_tensor(out=ki_[ti], in0=kbT[:, cs], in1=Efi[ti], op=ALU.mult), reads=[KB, EFI[ti]], writes=[KI[ti]])
                    P.op("dve", lambda e: e.tensor_tensor(out=ke[ti], in0=kps, in1=Wd[ti], op=ALU.mult), reads=[PSB[fa], WD[ti]], writes=[KE[ti]])

                def s2(cn):
                    ti = cn % NTMP
                    fa = 2 + (cn % 2)
                    fb = 4 + (cn % 2)
                    cs = slice(cn * 128, (cn + 1) * 128)
                    P.op("pe", lambda e: e.matmul(PS[fb][:, 256:384], lhsT=ki_[ti], rhs=qd[ti], start=True, stop=True), reads=[KI[ti], QD[ti]], writes=[PSB[fb]])
                    P.op("pe", lambda e: e.matmul(PS[fb][:, 384:512], lhsT=kib[:, cs], rhs=qdb[:, cs], start=True, stop=True), reads=[KIB, QDB], writes=[PSB[fb]])
                    if cn < 31:
                        P.op("pe", lambda e: e.matmul(PS[fb][:, 0:256], lhsT=ke[ti], rhs=vB3[:, cn, :], start=True, stop=True), reads=[KE[ti], VB[cn // 2]], writes=[PSB[fb]])
                    P.op("dve", lambda e: e.tensor_tensor(out=ABm[ti], in0=PS[fb][:, 256:512], in1=cb[:, CB_M2:CB_M2 + 256], op=ALU.mult),
                         reads=[PSB[fb], CBB], writes=[AB[ti]])
                    if cn < 31:
                        P.op("dve", lambda e: e.scalar_tensor_tensor(out=Sst2[cn % 2], in0=Sst2[(cn + 1) % 2], scalar=Ef[ti][:, 127:128], in1=PS[fb][:, 0:256],
                                                                     op0=ALU.mult, op1=ALU.add),
                             reads=[SST2[(cn + 1) % 2], EF[ti], PSB[fb]], writes=[SST2[cn % 2]])

                def s2c(cn):
                    if cn < 31:
                        P.op("act", lambda e: e.activation(out=Sring[(cn + 1) % 3], in_=Sst2[cn % 2], func=AF.Copy), reads=[SST2[cn % 2]], writes=[SR[(cn + 1) % 3]])

                def s3(cn):
                    ti = cn % NTMP
                    bz = 6 + (cn % 2)
                    b2 = cn % 2
                    cs = slice(cn * 128, (cn + 1) * 128)
                    blk = cn // 4
                    zb = blk % 2
                    for hf in range(2):
                        mms = [(vB3[:, cn, hf * 128:(hf + 1) * 128], ABm[ti][:, 0:128], [VB[cn // 2], AB[ti]]),
                               (vB3[:, cn, hf * 128:(hf + 1) * 128], ABm[ti][:, 128:256], [VB[cn // 2], AB[ti]])]
                        if cn > 0:
                            mms.append((Sring[cn % 3][:, hf * 128:(hf + 1) * 128], qd[ti], [SR[cn % 3], QD[ti]]))
                        if cn < 31:
                            mms.append((Sb3[:, cn, hf * 128:(hf + 1) * 128], qdb[:, cs], [SBB, QDB]))
                        for mi, (lh, rh, rd) in enumerate(mms):
                            P.op("pe", lambda e, lh=lh, rh=rh, mi=mi, hf=hf, nm=len(mms): e.matmul(PS[bz][:, hf * 128:(hf + 1) * 128], lhsT=lh, rhs=rh,
                                                                                                 start=(mi == 0), stop=(mi == nm - 1)),
                                 reads=rd, writes=[PSB[bz]])
                    P.op("act", lambda e: e.activation(out=sq[b2], in_=PS[bz][:, 0:256], func=AF.Square), reads=[PSB[bz]], writes=[SQ[b2]])

                def s4pe(cn):
                    bz = 6 + (cn % 2)
                    b2 = cn % 2
                    for hf in range(2):
                        P.op("pe", lambda e, hf=hf: e.matmul(PS[bz][:, 256:384], lhsT=ones, rhs=sq[b2][:, hf * 128:(hf + 1) * 128], start=(hf == 0), stop=(hf == 1)),
                             reads=[CBB, SQ[b2]], writes=[PSB[bz]])

                def s4(cn):
                    bz = 6 + (cn % 2)
                    b2 = cn % 2
                    blk = cn // 4
                    zb = blk % 2
                    P.op("act", lambda e: e.activation(out=lnv[b2], in_=PS[bz][:, 256:384], func=AF.Ln, scale=1.0 / 256, bias=EPS), reads=[PSB[bz]], writes=[LNV[b2]])
                    P.op("act", lambda e: e.activation(out=rstd[b2], in_=lnv[b2], func=AF.Exp, scale=-0.5), reads=[LNV[b2]], writes=[RSTD[b2]])
                    for hf in range(2):
                        P.op("dve", lambda e, hf=hf: e.scalar_tensor_tensor(out=tt_[b2][:, hf * 128:(hf + 1) * 128], in0=PS[bz][:, hf * 128:(hf + 1) * 128],
                                                                            scalar=gon[:, hf:hf + 1], in1=rstd[b2], op0=ALU.mult, op1=ALU.mult),
                             reads=[PSB[bz], RSTD[b2], GON], writes=[TT[b2][hf]])
                        c0 = hf * 512 + (cn % 4) * 128
                        P.op("pool", lambda e, hf=hf, c0=c0: e.tensor_tensor(out=ybst[zb][:, c0:c0 + 128], in0=tt_[b2][:, hf * 128:(hf + 1) * 128],
                                                                            in1=szb[zb][:, c0:c0 + 128], op=ALU.mult),
                             reads=[TT[b2][hf], SZB[zb]], writes=[YBST[zb]])
                    if cn % 4 == 3:
                        for hf in range(2):
                            P.dma("sp", lambda q, hf=hf: q.dma_start(out=yb_d[h * 2 + hf, :, blk * 512:(blk + 1) * 512], in_=ybst[zb][:, hf * 512:(hf + 1) * 512]),
                                  ybsem[zb][hf], reads=[YBST[zb]], writes=[YBD[h * 2 + hf]])

                for st in range(32 + 3):
                    if 0 <= st - 3 < 32:
                        s4pe(st - 3)
                    if st < 32:
                        s1(st)
                    if 0 <= st - 2 < 32:
                        s2c(st - 2)
                    if 0 <= st - 1 < 32:
                        s2(st - 1)
                    if 0 <= st - 3 < 32:
                        s4(st - 3)
                    if 0 <= st - 2 < 32:
                        s3(st - 2)
            for h_ in range(4):
                gla_head(h_)
            P.barrier()
            if STOP in ("C", "C0", "C1", "C2"):
                return YAD, YBD

            a_reset()
            wa = a_bf(4 * 1024)
            wb = a_bf(8 * 1024)
            wo = a_bf(8 * 1024)
            wa3 = wa.rearrange("p (k c) -> p k c", c=1024)
            wb3 = wb.rearrange("p (k c) -> p k c", c=1024)
            wo3 = wo.rearrange("p (k c) -> p k c", c=1024)
            wpost = a_f32(1024)
            WA, WBb, WO, WPOST = Buf("wa"), Buf("wb"), Buf("wo"), Buf("wpost")
            P.dma("pool", lambda q: q.dma_start(out=wa3, in_=wa_d[l]), msem(), writes=[WA])
            for k2 in range(2):
                P.dma("pool", lambda q, k2=k2: q.dma_start(out=wb3[:, k2 * 4:(k2 + 1) * 4, :], in_=wb_d[l, :, k2 * 4:(k2 + 1) * 4, :]), msem(), writes=[WBb])
                P.dma("pool", lambda q, k2=k2: q.dma_start(out=wo3[:, k2 * 4:(k2 + 1) * 4, :], in_=wo_d[l, :, k2 * 4:(k2 + 1) * 4, :]), msem(), writes=[WO])
            P.dma("sp", lambda q: q.dma_start(out=wpost, in_=npost_d[l:l + 1, :].partition_broadcast(128)), msem(), writes=[WPOST])
            NB = 256
            yab = [a_bf(4 * NB) for _ in range(2)]
            ybb = [a_bf(8 * NB) for _ in range(2)]
            sgb = [a_bf(16 * NB) for _ in range(2)]
            mrg = [a_bf(8 * NB) for _ in range(2)]
            tA = [a_f32(NB) for _ in range(2)]
            tB = [a_f32(NB) for _ in range(2)]
            xr = [a_f32(1024) for _ in range(4)]
            res = [a_f32(1024) for _ in range(2)]
            jk2 = a_bf(512)
            YAB = [Buf("yab0"), Buf("yab1")]
            YBB = [[Buf("ybb%d_%d" % (i, j)) for j in range(2)] for i in range(2)]
            SGB = [[Buf("sgb%d_%d" % (i, j)) for j in range(4)] for i in range(2)]
            MRG = [Buf("mrg0"), Buf("mrg1")]
            TA = [Buf("tA0"), Buf("tA1")]
            TB_ = [Buf("tB0"), Buf("tB1")]
            XR = [Buf("xr%d" % i) for i in range(4)]
            RES = [Buf("res0"), Buf("res1")]
            JK2 = Buf("jk2")
            SM2s = [Buf("small20"), Buf("small21")]
            P.op("pool", lambda e: e.memset(small[:, 0:64], 0.0), writes=SM2s)
            ldsem = [[msem() for _ in range(7)] for _ in range(2)]
            xrsem = [msem(), msem(), msem(), msem()]
            ossem = [msem(), msem()]
            nblk = S // NB

            def ld_blk(blk):
                b = blk % 2
                ts = slice(blk * NB, (blk + 1) * NB)
                P.dma("sp", lambda q: q.dma_start(out=yab[b].rearrange("p (k t) -> p k t", t=NB), in_=ya_d[:, :, ts].rearrange("k p t -> p k t")),
                      ldsem[b][0], reads=YAD, writes=[YAB[b]])
                for pc in range(2):
                    P.dma("sp", lambda q, pc=pc: q.dma_start(out=ybb[b][:, pc * 4 * NB:(pc + 1) * 4 * NB].rearrange("p (k t) -> p k t", t=NB),
                                                            in_=yb_d[pc * 4:(pc + 1) * 4, :, ts].rearrange("k p t -> p k t")),
                          ldsem[b][1 + pc], reads=YBD, writes=[YBB[b][pc]])
                for pc in range(4):
                    P.dma("sp", lambda q, pc=pc: q.dma_start(out=sgb[b][:, pc * 4 * NB:(pc + 1) * 4 * NB].rearrange("p (k t) -> p k t", t=NB),
                                                            in_=sg_d[pc * 4:(pc + 1) * 4, :, ts].rearrange("k p t -> p k t")),
                          ldsem[b][3 + pc], reads=SGD, writes=[SGB[b][pc]])

            def ld_x(tok):
                xb = tok % 4
                P.dma("sp", lambda q: q.dma_start(out=xr[xb], in_=x_in[tok * 128:(tok + 1) * 128, :]), xrsem[xb], reads=[XIN], writes=[XR[xb]])

            ld_blk(0)
            ld_x(0)
            ld_x(1)
            def branch(blk):
                b = blk % 2
                ts = slice(blk * NB, (blk + 1) * NB)
                if blk + 1 < nblk:
                    ld_blk(blk + 1)
                for c in range(8):
                    pa = next_pj([0, 1, 2, 3])
                    for kc in range(4):
                        P.op("pe", lambda e, pa=pa, kc=kc, c=c, b=b: e.matmul(PS[pa][:, 0:NB], lhsT=wa3[:, kc, c * 128:(c + 1) * 128], rhs=yab[b][:, kc * NB:(kc + 1) * NB],
                                                                              start=(kc == 0), stop=(kc == 3)), reads=[WA, YAB[b]], writes=[PSB[pa]])
                    for kc in range(8):
                        P.op("pe", lambda e, pa=pa, kc=kc, c=c, b=b: e.matmul(PS[pa][:, NB:2 * NB], lhsT=wb3[:, kc, c * 128:(c + 1) * 128], rhs=ybb[b][:, kc * NB:(kc + 1) * NB],
                                                                              start=(kc == 0), stop=(kc == 7)), reads=[WBb, YBB[b][kc // 4]], writes=[PSB[pa]])
                    tb = c % 2
                    P.op("dve", lambda e, pa=pa, c=c, b=b, tb=tb: e.tensor_tensor(out=tA[tb], in0=PS[pa][:, 0:NB], in1=sgb[b][:, c * NB:(c + 1) * NB], op=ALU.mult),
                         reads=[PSB[pa], SGB[b][c // 4]], writes=[TA[tb]])
                    P.op("dve", lambda e, pa=pa, c=c, b=b, tb=tb: e.tensor_tensor(out=tB[tb], in0=PS[pa][:, NB:2 * NB], in1=sgb[b][:, (8 + c) * NB:(9 + c) * NB], op=ALU.mult),
                         reads=[PSB[pa], SGB[b][2 + c // 4]], writes=[TB_[tb]])
                    P.op("pool", lambda e, c=c, b=b, tb=tb: e.tensor_tensor(out=mrg[b][:, c * NB:(c + 1) * NB], in0=tA[tb], in1=tB[tb], op=ALU.add),
                         reads=[TA[tb], TB_[tb]], writes=[MRG[b]])
            def outp(blk):
                b = blk % 2
                for tt in range(NB // 128):
                    tok = blk * (NB // 128) + tt
                    xb = tok % 2
                    xq = tok % 4
                    if tok + 2 < NT:
                        ld_x(tok + 2)
                    pos_ = [4 + 2 * (tok % 2), 5 + 2 * (tok % 2)]
                    for half in range(2):
                        po = pos_[half]
                        for kc in range(8):
                            P.op("pe", lambda e, po=po, kc=kc, half=half, tt=tt, b=b: e.matmul(
                                PS[po][:, 0:512], lhsT=mrg[b][:, kc * NB + tt * 128:kc * NB + (tt + 1) * 128], rhs=wo3[:, kc, half * 512:(half + 1) * 512],
                                start=(kc == 0), stop=(kc == 7)), reads=[MRG[b], WO], writes=[PSB[po]])
                        P.op("act", lambda e, po=po, tok=tok, half=half: e.activation(out=jk2, in_=PS[po][:, 0:512], func=AF.Square,
                                                                                     accum_out=small[:, 2 * tok + half:2 * tok + half + 1]),
                             reads=[PSB[po]], writes=[JK2, SM2s[tok % 2]])
                    c_ss = 208 + (tok % 2) * 4
                    P.op("dve", lambda e, tok=tok, c_ss=c_ss: e.tensor_tensor(out=small[:, c_ss:c_ss + 1], in0=small[:, 2 * tok:2 * tok + 1], in1=small[:, 2 * tok + 1:2 * tok + 2], op=ALU.add),
                         reads=[SM2s[tok % 2]], writes=[SM2s[tok % 2]])
                    P.op("act", lambda e, c_ss=c_ss: e.activation(out=small[:, c_ss + 1:c_ss + 2], in_=small[:, c_ss:c_ss + 1], func=AF.Ln, scale=1.0 / D, bias=EPS),
                         reads=[SM2s[tok % 2]], writes=[SM2s[tok % 2]])
                    P.op("act", lambda e, c_ss=c_ss: e.activation(out=small[:, c_ss + 2:c_ss + 3], in_=small[:, c_ss + 1:c_ss + 2], func=AF.Exp, scale=-0.5),
                         reads=[SM2s[tok % 2]], writes=[SM2s[tok % 2]])
                    for half in range(2):
                        po = pos_[half]
                        hs = slice(half * 512, (half + 1) * 512)
                        P.op("dve", lambda e, po=po, hs=hs, xb=xb, c_ss=c_ss: e.scalar_tensor_tensor(out=res[xb][:, hs], in0=PS[po][:, 0:512], scalar=small[:, c_ss + 2:c_ss + 3],
                                                                                                  in1=wpost[:, hs], op0=ALU.mult, op1=ALU.mult),
                             reads=[PSB[po], SM2s[tok % 2], WPOST], writes=[RES[xb]])
                        P.op("pool", lambda e, hs=hs, xb=xb, xq=xq: e.tensor_tensor(out=res[xb][:, hs], in0=res[xb][:, hs], in1=xr[xq][:, hs], op=ALU.add),
                             reads=[RES[xb], XR[xq]], writes=[RES[xb]])
                    P.dma("sp", lambda q, tok=tok, xb=xb: q.dma_start(out=x_out[tok * 128:(tok + 1) * 128, :], in_=res[xb]), ossem[xb], reads=[RES[xb]], writes=[XOUT])
            for blk in range(nblk + 1):
                if blk < nblk:
                    branch(blk)
                if blk >= 1:
                    outp(blk - 1)
            P.barrier()
            return YAD, YBD

        XB0, XB1, XB2 = Buf("x_in"), Buf("x_mid"), Buf("x_out")
        nl = NL_RUN
        if nl == 1:
            yad, ybd = layer(0, x_d, y_d, XB0, XB2)
        else:
            yad, ybd = layer(0, x_d, x1_d, XB0, XB1)
            yad, ybd = layer(1, x1_d, y_d, XB1, XB2)
        if debug:
            dsem = P.dma_sem()
            DB = Buf("dbg")
            for h in range(4):
                P.dma("sp", lambda q, h=h: q.dma_start(out=dbg["ya"][h], in_=ya_d[h]), dsem, reads=yad, writes=[DB])
            for h in range(8):
                P.dma("sp", lambda q, h=h: q.dma_start(out=dbg["yb"][h], in_=yb_d[h]), dsem, reads=ybd, writes=[DB])
            for k in range(8):
                P.dma("sp", lambda q, k=k: q.dma_start(out=dbg["h"][:, k, :], in_=hT[:, k, :]), dsem, reads=[HT], writes=[DB])
            for j in range(6):
                P.dma("sp", lambda q, j=j: q.dma_start(out=dbg["t2"][j], in_=t2_d[j]), dsem, writes=[DB])
        P.barrier()
        P.emit()
    return nc


NL_RUN = NL
FULL_SYNC = True
STOP = None


def _prep(inputs):
    f32 = np.float32
    w_in = np.asarray(inputs["w_in"], f32)
    wp = np.zeros((NL, NCT, 128, 8, 128), f32)
    for l in range(NL):
        wz = np.concatenate([w_in[l], np.zeros((D, 1), f32)], axis=1)
        sel = wz[:, CT_COLS.reshape(-1)].reshape(8, 128, NCT, 128)
        wp[l] = sel.transpose(2, 1, 0, 3)
    wa = np.ascontiguousarray(np.asarray(inputs["w_branch_a"], f32).reshape(NL, 4, 128, 1024).transpose(0, 2, 1, 3))
    wb = np.ascontiguousarray(np.asarray(inputs["w_branch_b"], f32).reshape(NL, 8, 128, 1024).transpose(0, 2, 1, 3))
    wo = np.ascontiguousarray(np.asarray(inputs["w_out"], f32).reshape(NL, 8, 128, 1024).transpose(0, 2, 1, 3))
    gup = np.zeros((NL, 64, 512), f32)
    gup[:, 0:16] = np.asarray(inputs["gate_up_fwd"], f32)
    gup[:, 16] = np.asarray(inputs["gate_bias_fwd"], f32)
    gup[:, 32:48] = np.asarray(inputs["gate_up_bwd"], f32)
    gup[:, 48] = np.asarray(inputs["gate_bias_bwd"], f32)
    gon = np.ascontiguousarray(np.asarray(inputs["gla_out_norm"], f32).reshape(NL, 2, 128).transpose(0, 2, 1))
    cbc, ctab, stab = _consts()
    shared = {"wp": wp, "wa": wa, "wb": wb, "wo": wo, "gup": gup,
              "npre": np.ascontiguousarray(np.asarray(inputs["norm_pre"], f32)),
              "npost": np.ascontiguousarray(np.asarray(inputs["norm_post"], f32)),
              "gon": gon, "cb": cbc, "ctab": ctab, "stab": stab}
    return shared


def kernel(**inputs):
    x = np.ascontiguousarray(np.asarray(inputs["x"], np.float32))
    shared = _prep(inputs)
    nc = build(debug=False)
    in_maps = [dict(shared, x=x[b]) for b in range(8)]
    res = run_bass_kernel_spmd(nc, in_maps, core_ids=list(range(8)))
    return np.stack([np.asarray(r["y"], np.float32) for r in res.results], axis=0)
```

```python
import numpy as np
import ml_dtypes
import concourse.bass as bass
import concourse.mybir as mybir
from concourse.bass_utils import run_bass_kernel_spmd
from contextlib import ExitStack

F32 = mybir.dt.float32
BF16 = mybir.dt.bfloat16
AF = mybir.ActivationFunctionType
ALU = mybir.AluOpType

S = 4096
D = 1024
NL = 2
NT = 32
EPS = 1e-6
GROUPS = (1, 4, 16)
DEBUG = False

CT = {}


def _build_ct():
    cols = []

    def add(name, c):
        c = list(c) + [-1] * (128 - len(c))
        CT[name] = len(cols)
        cols.append(c)

    perm = list(range(32, 128)) + list(range(0, 32))
    for g in range(3):
        for qk in range(2):
            for h in range(4):
                base = g * 1536 + qk * 512 + h * 128
                add(("a", g, qk, h), [base + p for p in perm])
        for h in range(4):
            base = g * 1536 + 1024 + h * 128
            add(("av", g, h), range(base, base + 128))
    for g in range(3):
        for qk in range(2):
            c = []
            c2 = []
            for h in range(4):
                base = g * 1536 + qk * 512 + h * 128
                c += [base + 16 + i for i in range(16)] + [base + i for i in range(16)]
                c2 += [base + i for i in range(32)]
            add(("sw", g, qk), c)
            add(("ar", g, qk), c2)
    for h in range(4):
        add(("za", h), range(4608 + h * 128, 4608 + (h + 1) * 128))
        add(("qb", h), range(5120 + h * 128, 5120 + (h + 1) * 128))
        add(("kb", h), range(5632 + h * 128, 5632 + (h + 1) * 128))
        for hf in range(2):
            b = 6144 + h * 256 + hf * 128
            add(("vb", h, hf), range(b, b + 128))
            b = 7168 + h * 256 + hf * 128
            add(("zb", h, hf), range(b, b + 128))
    for c8 in range(16):
        add(("gate", c8), range(8192 + c8 * 128, 8192 + (c8 + 1) * 128))
    lr = [-1] * 128
    for i in range(16):
        lr[i] = 10240 + i
        lr[32 + i] = 10256 + i
    add(("lr",), lr)
    return np.array(cols, dtype=np.int64)


CT_COLS = _build_ct()
NCT = CT_COLS.shape[0]

CB_ID, CB_ONES, CB_M1, CB_M2, CB_M3, CB_TF, CB_TB, CB_SU, CB_SL, CB_TFN, CB_TBN = [i * 128 for i in range(11)]
NCB = 11 * 128


def _consts():
    i = np.arange(128)[:, None]
    j = np.arange(128)[None, :]
    cb = np.zeros((128, NCB), np.float32)
    cb[:, CB_ID:CB_ID + 128] = (i == j)
    cb[:, CB_ONES:CB_ONES + 128] = 1.0
    cb[:, CB_M1:CB_M1 + 128] = (i >= j)
    cb[:, CB_M2:CB_M2 + 128] = (i <= j)
    cb[:, CB_M3:CB_M3 + 128] = (i > j)
    cb[:, CB_TF:CB_TF + 128] = (i <= j) * (-1.0 / 16)
    cb[:, CB_TB:CB_TB + 128] = (i >= j) * (-1.0 / 16)
    cb[:, CB_SU:CB_SU + 128] = (i > j) * (-1.0 / 16)
    cb[:, CB_SL:CB_SL + 128] = (i < j) * (-1.0 / 16)
    cb[:, CB_TFN:CB_TFN + 128] = (i <= j) * (1.0 / 16)
    cb[:, CB_TBN:CB_TBN + 128] = (i >= j) * (1.0 / 16)
    inv_freq = (np.float32(500000.0) ** (-np.arange(0, 32, 2, dtype=np.float32) / np.float32(32))).astype(np.float32)
    ang = np.arange(S, dtype=np.float32)[None, :] * inv_freq[:, None]
    cos = np.cos(ang.astype(np.float64))
    sin = np.sin(ang.astype(np.float64))
    ctab = np.concatenate([cos, cos] * 4, axis=0)
    s32 = np.concatenate([-sin, sin], axis=0)
    stab = np.concatenate([s32] * 4, axis=0)
    bf = ml_dtypes.bfloat16
    return cb.astype(bf), ctab.astype(np.float32), stab.astype(np.float32)


class Buf:
    __slots__ = ("name", "w", "r", "excl")

    def __init__(self, name, excl=False):
        self.name = name
        self.w = None
        self.r = {}
        self.excl = excl


class Prog:
    ENG = ("pe", "act", "dve", "pool", "sp")

    def __init__(self, nc, es):
        self.nc = nc
        self.es = es
        self.ops = {e: [] for e in self.ENG}
        self.cnt = {e: 0 for e in self.ENG}
        self.sem = {e: es.enter_context(nc.semaphore("s_" + e)) for e in self.ENG}
        self.seen = {e: {} for e in self.ENG}
        self.dsems = []

    def dma_sem(self, name=None):
        s = self.es.enter_context(self.nc.semaphore(name or ("dq%d" % len(self.dsems))))
        d = [s, 0]
        self.dsems.append(d)
        return d

    def _waits(self, e, reads, writes):
        evs = []
        own = self.sem[e]
        same_ok = e in ("act", "dve", "pool")
        for b in reads:
            if b.w is not None and (b.w[0] is not own or same_ok):
                evs.append(b.w)
            if b.excl:
                for ev in b.r.values():
                    if ev[0] is not own:
                        evs.append(ev)
        for b in writes:
            if b.w is not None and (b.w[0] is not own or (same_ok and FULL_SYNC)):
                evs.append(b.w)
            for ev in b.r.values():
                if ev[0] is not own or (same_ok and FULL_SYNC):
                    evs.append(ev)
        waits = {}
        seen = self.seen[e]
        for (sem, val) in evs:
            k = id(sem)
            if seen.get(k, 0) >= val:
                continue
            if k in waits and waits[k][1] >= val:
                continue
            waits[k] = (sem, val)
        for k, (sem, val) in waits.items():
            seen[k] = val
        return list(waits.values())

    def _record(self, ev, reads, writes):
        k = id(ev[0])
        for b in reads:
            b.r[k] = ev
        for b in writes:
            b.w = ev
            b.r = {}

    def op(self, e, fn, reads=(), writes=()):
        waits = self._waits(e, reads, writes)
        self.cnt[e] += 1
        ev = (self.sem[e], self.cnt[e])
        self.ops[e].append((waits, fn, ev[0], 1))
        self._record(ev, reads, writes)
        return ev

    def dma(self, q, fn, dsem, reads=(), writes=()):
        waits = self._waits(q, reads, writes)
        dsem[1] += 16
        ev = (dsem[0], dsem[1])
        self.ops[q].append((waits, fn, dsem[0], 16))
        self._record(ev, reads, writes)
        return ev

    def barrier(self):
        evs = [(self.sem[e], self.cnt[e]) for e in self.ENG if self.cnt[e] > 0]
        evs += [(d[0], d[1]) for d in self.dsems if d[1] > 0]
        for e in self.ENG:
            waits = []
            for (sem, val) in evs:
                if sem is self.sem[e]:
                    continue
                k = id(sem)
                if self.seen[e].get(k, 0) >= val:
                    continue
                self.seen[e][k] = val
                waits.append((sem, val))
            if waits:
                self.ops[e].append((waits, None, None, 0))

    def emit(self):
        nc = self.nc
        engs = {"pe": "tensor", "act": "scalar", "dve": "vector", "pool": "gpsimd", "sp": "sync"}
        with nc.Block() as block:
            for e in self.ENG:
                ops = self.ops[e]

                def body(eng, ops=ops):
                    for (waits, fn, sem, inc) in ops:
                        for (s, v) in waits:
                            eng.wait_ge(s, v)
                        if fn is not None:
                            fn(eng).then_inc(sem, inc)

                getattr(block, engs[e])(body)


def build(debug=False):
    nc = bass.Bass("TRN2", target_bir_lowering=False)

    def din(name, shape, dt):
        return nc.dram_tensor(name, shape, dt, kind="ExternalInput").ap()

    def dscr(name, shape, dt):
        return nc.dram_tensor(name, shape, dt, kind="Internal").ap()

    x_d = din("x", [S, D], F32)
    wp_d = din("wp", [NL, NCT, 128, 8, 128], F32)
    wa_d = din("wa", [NL, 128, 4, 1024], F32)
    wb_d = din("wb", [NL, 128, 8, 1024], F32)
    wo_d = din("wo", [NL, 128, 8, 1024], F32)
    gup_d = din("gup", [NL, 64, 512], F32)
    npre_d = din("npre", [NL, 1024], F32)
    npost_d = din("npost", [NL, 1024], F32)
    gon_d = din("gon", [NL, 128, 2], F32)
    cb_d = din("cb", [128, NCB], BF16)
    ctab_d = din("ctab", [128, S], F32)
    stab_d = din("stab", [128, S], F32)
    y_d = nc.dram_tensor("y", [S, D], F32, kind="ExternalOutput").ap()
    t2_d = dscr("t2s", [6, 128, S], BF16)
    sg_d = dscr("sgs", [16, 128, S], BF16)
    ya_d = dscr("yas", [4, 128, S], BF16)
    yb_d = dscr("ybs", [8, 128, S], BF16)
    szb_d = dscr("szbs", [8, 128, S], BF16)
    x1_d = dscr("x1s", [S, D], F32)
    dbg = {}
    if debug:
        dbg["ya"] = nc.dram_tensor("dbg_ya", [4, 128, S], BF16, kind="ExternalOutput").ap()
        dbg["yb"] = nc.dram_tensor("dbg_yb", [8, 128, S], BF16, kind="ExternalOutput").ap()
        dbg["h"] = nc.dram_tensor("dbg_h", [128, 8, S], BF16, kind="ExternalOutput").ap()
        for nm in ("q", "k", "v"):
            dbg[nm] = nc.dram_tensor("dbg_" + nm, [128, S], BF16, kind="ExternalOutput").ap()
        dbg["oa"] = nc.dram_tensor("dbg_oa", [128, S], F32, kind="ExternalOutput").ap()
        dbg["t2"] = nc.dram_tensor("dbg_t2", [6, 128, S], BF16, kind="ExternalOutput").ap()

    with ExitStack() as es:
        P = Prog(nc, es)

        def sbuf(name, shape, dt):
            return es.enter_context(nc.sbuf_tensor("sb_" + name, shape, dt))

        hT = sbuf("hT", [128, 8, S], BF16)
        NW = 6
        wts = sbuf("wts", [128, NW, 8, 128], BF16)
        cb = sbuf("cb", [128, NCB], BF16)
        AR = 60 * 1024
        arena = sbuf("arena", [128, AR], BF16)
        small = sbuf("small", [128, 256], F32)
        PS = [es.enter_context(nc.psum_tensor("ps%d" % i, [128, 512], F32)) for i in range(8)]
        PSB = [Buf("ps%d" % i, excl=True) for i in range(8)]

        HT = Buf("hT")
        CBB = Buf("cb")
        WB = [Buf("w%d" % i) for i in range(NW)]
        wsem = [P.dma_sem() for _ in range(NW)]
        sem_c = P.dma_sem()
        state = {"wslot": 0, "misc": 0, "aoff": 0, "lay": 0}

        named = {}

        def msem():
            import sys
            f = sys._getframe(1)
            key = (f.f_lineno, named.setdefault(("cnt", f.f_lineno, state["lay"]), [0])[0])
            named[("cnt", f.f_lineno, state["lay"])][0] += 1
            if key not in named:
                named[key] = P.dma_sem()
            return named[key]

        def a_reset():
            state["aoff"] = 0

        def a_bf(n):
            o = state["aoff"]
            assert o + n <= AR, ("arena overflow", o, n)
            state["aoff"] = o + n
            return arena[:, o:o + n]

        def a_f32(n):
            o = state["aoff"]
            o += o % 2
            assert o + 2 * n <= AR, ("arena overflow", o, n)
            state["aoff"] = o + 2 * n
            return arena[:, o:o + 2 * n].bitcast(F32)

        ident = cb[:, CB_ID:CB_ID + 128]
        ones = cb[:, CB_ONES:CB_ONES + 128]

        P.dma("sp", lambda q: q.dma_start(out=cb[:], in_=cb_d), sem_c, writes=[CBB])

        def load_wt(l, ct):
            slot = state["wslot"]
            state["wslot"] = (slot + 1) % NW
            P.dma("pool", lambda q: q.dma_start(out=wts[:, slot], in_=wp_d[l, ct]), wsem[slot], writes=[WB[slot]])
            return slot

        class WStream:
            def __init__(self, l, cts, ahead=3):
                self.l, self.cts, self.ahead = l, cts, ahead
                self.slots = {}
                self.nxt = 0

            def get(self, i):
                while self.nxt < len(self.cts) and self.nxt <= i + self.ahead:
                    self.slots[self.nxt] = load_wt(self.l, self.cts[self.nxt])
                    self.nxt += 1
                return self.slots[i]

        pjr = {"i": 0}

        def next_pj(banks):
            pjr["i"] = (pjr["i"] + 1) % len(banks)
            return banks[pjr["i"]]

        def rhs_class(d, kc, blk):
            if d == 1:
                return hT[:, kc, blk * 512:(blk + 1) * 512]
            v = hT[:, kc, :].rearrange("p (t r) -> p r t", r=d)
            Lc = S // d
            if Lc >= 512:
                r = (blk * 512) // Lc
                t0 = (blk * 512) % Lc
                return v[:, r, t0:t0 + 512]
            nr = 512 // Lc
            return v[:, blk * nr:(blk + 1) * nr, :]

        def tab_class(tab, p0, p1, d, blk):
            if d == 1:
                return tab[p0:p1, blk * 512:(blk + 1) * 512]
            v = tab[p0:p1, :].rearrange("p (t r) -> p r t", r=d)
            Lc = S // d
            if Lc >= 512:
                r = (blk * 512) // Lc
                t0 = (blk * 512) % Lc
                return v[:, r, t0:t0 + 512]
            nr = 512 // Lc
            return v[:, blk * nr:(blk + 1) * nr, :]

        def shp(ap, d):
            Lc = S // d
            if d == 1 or Lc >= 512:
                return ap
            return ap.rearrange("p (r t) -> p r t", t=Lc)

        def proj_fm(slot, rhs_fn, evac_fn, banks, nblk=8, d=1):
            for blk in range(nblk):
                pi = next_pj(banks)
                for kc in range(8):
                    P.op("pe", lambda e, pi=pi, kc=kc, blk=blk: e.matmul(
                        shp(PS[pi][:, 0:512], d), lhsT=wts[:, slot, kc, :], rhs=rhs_fn(kc, blk),
                        start=(kc == 0), stop=(kc == 7)), reads=[WB[slot], HT], writes=[PSB[pi]])
                evac_fn(blk, pi)

        cpr = {"i": 0}

        DEF_ENGS = ("dve", "act", "dve", "act", "pool", "dve", "act", "pool")

        def stage_blk(d, ring, RINGB, blk, engs=DEF_ENGS):
            if d == 1:
                return
            nr = len(ring)
            for kc in range(8):
                ri = (blk * 8 + kc) % nr
                eng = engs[kc]
                if eng == "act":
                    P.op("act", lambda e, ri=ri, kc=kc: e.activation(out=shp(ring[ri], d), in_=rhs_class(d, kc, blk), func=AF.Copy), reads=[HT], writes=[RINGB[ri]])
                else:
                    P.op(eng, lambda e, ri=ri, kc=kc: e.tensor_copy(out=shp(ring[ri], d), in_=rhs_class(d, kc, blk)), reads=[HT], writes=[RINGB[ri]])

        def proj_multi(slots, evac_fns, banksets, d, ring, RINGB, engs=DEF_ENGS, prestaged=False):
            nr = len(ring)

            def stage(blk):
                stage_blk(d, ring, RINGB, blk, engs)

            if not prestaged:
                stage(0)
            for blk in range(8):
                bs = banksets[blk % len(banksets)]
                for kc in range(8):
                    ri = (blk * 8 + kc) % nr
                    for t, slot in enumerate(slots):
                        if d == 1:
                            P.op("pe", lambda e, t=t, slot=slot, kc=kc, bs=bs, blk=blk: e.matmul(
                                PS[bs[t]][:, 0:512], lhsT=wts[:, slot, kc, :], rhs=rhs_class(1, kc, blk), start=(kc == 0), stop=(kc == 7)),
                                reads=[WB[slot], HT], writes=[PSB[bs[t]]])
                        else:
                            P.op("pe", lambda e, t=t, slot=slot, kc=kc, ri=ri, bs=bs: e.matmul(
                                PS[bs[t]][:, 0:512], lhsT=wts[:, slot, kc, :], rhs=ring[ri], start=(kc == 0), stop=(kc == 7)),
                                reads=[WB[slot], RINGB[ri]], writes=[PSB[bs[t]]])
                if blk < 7:
                    stage(blk + 1)
                for t, evf in enumerate(evac_fns):
                    evf(blk, bs[t])

        def layer(l, x_in, x_out, XIN, XOUT):
            state["lay"] = l
            a_reset()
            wpre = a_f32(1024)
            xts = [a_f32(1024) for _ in range(4)]
            hns = [a_bf(1024) for _ in range(2)]
            junks = [a_bf(1024) for _ in range(3)]
            XT = [Buf("xt%d" % i) for i in range(4)]
            HN = [Buf("hn0"), Buf("hn1")]
            JKs = [Buf("junk%d" % i) for i in range(3)]
            WPRE = Buf("wpre")
            SMs = [Buf("small0"), Buf("small1")]
            xsem = [msem(), msem(), msem(), msem()]
            P.op("pool", lambda e: e.memset(small[:, 0:64], 0.0), writes=SMs)
            P.dma("sp", lambda q: q.dma_start(out=wpre, in_=npre_d[l:l + 1, :].partition_broadcast(128)), msem(), writes=[WPRE])
            def ldx(i):
                bx = i % 4
                P.dma("sp", lambda q: q.dma_start(out=xts[bx], in_=x_in[i * 128:(i + 1) * 128, :]), xsem[bx], reads=[XIN], writes=[XT[bx]])

            def evac_hT(i):
                pi = 6 + (i % 2)
                psb = PS[pi][:, 0:512].bitcast(BF16)
                P.op("dve", lambda e: e.tensor_copy(out=hT[:, :, i * 128:(i + 1) * 128], in_=psb.rearrange("p (k c) -> p k c", c=128)),
                     reads=[PSB[pi]], writes=[Buf("hTt")])

            for i in range(3):
                ldx(i)
            for i in range(NT):
                b = i % 2
                bx = i % 4
                if i + 3 < NT:
                    ldx(i + 3)
                P.op("act", lambda e, i=i, bx=bx: e.activation(out=junks[i % 3], in_=xts[bx], func=AF.Square, accum_out=small[:, i:i + 1]),
                     reads=[XT[bx]], writes=[JKs[i % 3], SMs[b]])
                P.op("act", lambda e, i=i: e.activation(out=small[:, 64 + i:65 + i], in_=small[:, i:i + 1], func=AF.Ln, scale=1.0 / D, bias=EPS),
                     reads=[SMs[b]], writes=[SMs[b]])
                P.op("act", lambda e, i=i: e.activation(out=small[:, 128 + i:129 + i], in_=small[:, 64 + i:65 + i], func=AF.Exp, scale=-0.5),
                     reads=[SMs[b]], writes=[SMs[b]])
                P.op("dve", lambda e, i=i, b=b, bx=bx: e.scalar_tensor_tensor(out=hns[b], in0=xts[bx], scalar=small[:, 128 + i:129 + i], in1=wpre,
                                                                      op0=ALU.mult, op1=ALU.mult),
                     reads=[XT[bx], SMs[b], WPRE], writes=[HN[b]])
                pi = 6 + (i % 2)
                psb = PS[pi][:, 0:512].bitcast(BF16)
                for k in range(8):
                    P.op("pe", lambda e, k=k, b=b, psb=psb: e.transpose(psb[:, k * 128:(k + 1) * 128], hns[b][:, k * 128:(k + 1) * 128], ident),
                         reads=[HN[b], CBB], writes=[PSB[pi]])
                if i >= 1:
                    evac_hT(i - 1)
            evac_hT(NT - 1)
            P.barrier()
            if STOP == "A":
                return [], []

            a_reset()
            a_b0 = 0
            stab = a_f32(S)
            ctab = a_f32(S)
            STAB = Buf("stab")
            stg = [a_bf(S) for _ in range(2)]
            STG = [Buf("stg0"), Buf("stg1")]
            ring = [a_bf(512) for _ in range(8)]
            RINGB = [Buf("ring%d" % i) for i in range(8)]
            tmpA = [a_f32(S) for _ in range(2)]
            TMPA = [[Buf("tmpA%d_%d" % (i, j)) for j in range(8)] for i in range(2)]
            t2f = [a_f32(512) for _ in range(2)]
            T2F = [Buf("t2f0"), Buf("t2f1")]
            stgsem = [msem(), msem()]
            P.dma("sp", lambda q: q.dma_start(out=stab, in_=stab_d), msem(), writes=[STAB])
            CTB0 = Buf("ctab0")
            P.dma("sp", lambda q: q.dma_start(out=ctab, in_=ctab_d), msem(), writes=[CTB0])
            cts = []
            for g in range(3):
                cts += [CT[("ar", g, 0)], CT[("sw", g, 0)], CT[("ar", g, 1)], CT[("sw", g, 1)]]
            cts += [CT[("gate", c)] for c in range(16)]
            cts += [CT[("zb", h, hf)] for h in range(4) for hf in range(2)]
            ws = WStream(l, cts, ahead=1)
            SZD = [Buf("szd%d" % j) for j in range(8)]
            T2D = [Buf("t2d%d" % j) for j in range(6)]
            SGD = [Buf("sgd%d" % j) for j in range(16)]
            banks = [0, 1, 2, 3]
            n = 0
            rt = {"i": 0}
            for g in range(3):
                d = GROUPS[g]

                def mk_ar(d, qk):
                    def ev(blk, pi):
                        P.op("dve", lambda e: e.tensor_tensor(out=shp(tmpA[qk][:, blk * 512:(blk + 1) * 512], d), in0=shp(PS[pi][:, 0:512], d),
                                                             in1=tab_class(ctab, 0, 128, d, blk), op=ALU.mult),
                             reads=[PSB[pi], CTB0], writes=[TMPA[qk][blk]])
                    return ev

                def mk_sw(d, qk):
                    def ev(blk, pi):
                        tb = rt["i"] = (rt["i"] + 1) % 2
                        P.op("dve", lambda e: e.tensor_tensor(out=shp(t2f[tb], d), in0=shp(PS[pi][:, 0:512], d),
                                                             in1=tab_class(stab, 0, 128, d, blk), op=ALU.mult),
                             reads=[PSB[pi], STAB], writes=[T2F[tb]])
                        P.op("pool", lambda e: e.tensor_tensor(out=stg[qk][:, blk * 512:(blk + 1) * 512], in0=t2f[tb], in1=tmpA[qk][:, blk * 512:(blk + 1) * 512], op=ALU.add),
                             reads=[T2F[tb], TMPA[qk][blk]], writes=[STG[qk]])
                    return ev

                slots = [ws.get(n + i) for i in range(4)]
                n += 4
                proj_multi(slots, [mk_ar(d, 0), mk_sw(d, 0), mk_ar(d, 1), mk_sw(d, 1)], [[0, 1, 2, 3], [4, 5, 6, 7]], d, ring, RINGB,
                           engs=("dve", "act", "act", "dve", "act", "act", "dve", "act"))
                for qk in range(2):
                    j = g * 2 + qk
                    P.dma("sp", lambda q, j=j, qk=qk: q.dma_start(out=t2_d[j], in_=stg[qk]), stgsem[qk], reads=[STG[qk]], writes=[T2D[j]])
            for c in range(16):
                sb = n % 2
                slot = ws.get(n)

                def ev(blk, pi, sb=sb):
                    P.op("act", lambda e: e.activation(out=stg[sb][:, blk * 512:(blk + 1) * 512], in_=PS[pi][:, 0:512], func=AF.Sigmoid),
                         reads=[PSB[pi]], writes=[STG[sb]])

                proj_fm(slot, lambda kc, blk: rhs_class(1, kc, blk), ev, banks)
                P.dma("sp", lambda q, c=c, sb=sb: q.dma_start(out=sg_d[c], in_=stg[sb]), stgsem[sb], reads=[STG[sb]], writes=[SGD[c]])
                n += 1
            for c in range(8):
                sb = n % 2
                slot = ws.get(n)

                def ev(blk, pi, sb=sb):
                    P.op("act", lambda e: e.activation(out=stg[sb][:, blk * 512:(blk + 1) * 512], in_=PS[pi][:, 0:512], func=AF.Silu),
                         reads=[PSB[pi]], writes=[STG[sb]])

                proj_fm(slot, lambda kc, blk: rhs_class(1, kc, blk), ev, banks)
                P.dma("sp", lambda q, c=c, sb=sb: q.dma_start(out=szb_d[c], in_=stg[sb]), stgsem[sb], reads=[STG[sb]], writes=[SZD[c]])
                n += 1
            P.barrier()
            if STOP == "B0":
                return [], []

            state["aoff"] = a_b0
            qT = a_bf(S)
            kT = a_bf(S)
            vA = a_bf(S)
            vA3 = vA.rearrange("p (t c) -> p t c", c=128)
            acc = a_f32(2 * S)
            accU = acc[:, 0:S]
            accD = acc[:, S:2 * S]
            yast = a_bf(S)
            Es = [a_bf(256) for _ in range(4)]
            Pm = [a_bf(256) for _ in range(4)]
            szs = [a_f32(512) for _ in range(2)]
            vtb = [a_bf(512) for _ in range(2)]
            VTB = [Buf("vtb0"), Buf("vtb1")]
            ring = [a_bf(512) for _ in range(8)]
            RINGB = [Buf("ringb%d" % i) for i in range(8)]
            YAST = Buf("yast")
            QTR, KTR = Buf("qTR"), Buf("kTR")
            QTa = [Buf("qTa%d" % i) for i in range(8)]
            KTa = [Buf("kTa%d" % i) for i in range(8)]
            QT = QTa + [QTR]
            KT = KTa + [KTR]
            VA = [Buf("vA%d" % i) for i in range(8)]
            ACC = [Buf("acc%d" % i) for i in range(3)]
            ACCF = Buf("accF")
            EB = [Buf("E%d" % i) for i in range(4)]
            PMB = [Buf("Pm%d" % i) for i in range(4)]
            SZ = [Buf("sz0"), Buf("sz1")]
            t2sem = [msem(), msem()]
            yasem = msem()
            YAD = [Buf("yad%d" % h) for h in range(4)]
            cts = []
            for h in range(4):
                for g in range(3):
                    cts += [CT[("a", g, 0, h)], CT[("a", g, 1, h)], CT[("av", g, h)]]
                cts += [CT[("za", h)]]
            ws = WStream(l, cts)
            n = 0
            pjb = [0, 1, 2]
            rr = {"t": 0, "e": 0}
            for h in range(4):
                for g in range(3):
                    d = GROUPS[g]
                    Lc = S // d
                    evs = []
                    for qk, (dst, DSTa, DSTR) in enumerate(((qT, QTa, QTR), (kT, KTa, KTR))):
                        j = g * 2 + qk
                        P.dma("sp", lambda q, j=j, h=h, dst=dst: q.dma_start(out=dst[96:128, :], in_=t2_d[j, 32 * h:32 * h + 32, :]), t2sem[qk],
                              reads=[T2D[j]], writes=[DSTR])

                        def ev(blk, pi, dst=dst, DSTa=DSTa):
                            P.op("act", lambda e: e.activation(out=dst[0:96, blk * 512:(blk + 1) * 512], in_=PS[pi][0:96, 0:512], func=AF.Copy),
                                 reads=[PSB[pi]], writes=[DSTa[blk]])

                        evs.append(ev)

                    pend = []

                    def v_tr(blk):
                        vb_ = blk % 2
                        tbk = 3 + (blk % 2)
                        psb7 = PS[tbk][:, 0:256].bitcast(BF16)
                        for tt in range(4):
                            P.op("pe", lambda e, tt=tt: e.transpose(psb7[:, tt * 128:(tt + 1) * 128], vtb[vb_][:, tt * 128:(tt + 1) * 128], ident),
                                 reads=[VTB[vb_], CBB], writes=[PSB[tbk]])
                        P.op("dve", lambda e: e.tensor_copy(out=vA[:, blk * 512:(blk + 1) * 512], in_=psb7), reads=[PSB[tbk]], writes=[VA[blk]])

                    def ev_v(blk, pi):
                        vb_ = blk % 2
                        P.op("act", lambda e: e.activation(out=vtb[vb_], in_=PS[pi][:, 0:512], func=AF.Copy), reads=[PSB[pi]], writes=[VTB[vb_]])
                        if pend:
                            v_tr(pend.pop())
                        pend.append(blk)

                    evs.append(ev_v)
                    slots = [ws.get(n), ws.get(n + 1), ws.get(n + 2)]
                    n += 3
                    if d == 1:
                        for t in range(3):
                            proj_fm(slots[t], lambda kc, blk: rhs_class(1, kc, blk), evs[t], pjb)
                    else:
                        proj_multi(slots, evs, [[0, 1, 2], [5, 6, 7]], d, ring, RINGB, prestaged=True)
                    v_tr(pend.pop())
                    tiles = []
                    ntc = Lc // 128
                    for r in range(d):
                        for i in range(ntc + 1):
                            tq0 = max(0, 128 * i - 64)
                            tq1 = min(Lc, 128 * i + 64)
                            kts = []
                            if i >= 1:
                                kts.append((r * ntc + i - 1, CB_M1))
                            if i < ntc:
                                kts.append((r * ntc + i, CB_M2))
                            b0 = tq0 - (128 * i - 64)
                            tiles.append((r, tq0, tq1 - tq0, b0, kts))

                    def front(ti):
                        r, tq0, nq, b0, kts = tiles[ti]
                        sb = (3, 4, 0)[ti % 3]
                        ei = ti % 4
                        mq0 = r * Lc + tq0
                        for ki, (kt, mcol) in enumerate(kts):
                            P.op("pe", lambda e, ki=ki, kt=kt: e.matmul(PS[sb][:, ki * 128:ki * 128 + nq], lhsT=kT[:, kt * 128:(kt + 1) * 128],
                                                                       rhs=qT[:, mq0:mq0 + nq], start=True, stop=True),
                                 reads=KT + QT, writes=[PSB[sb]])
                        nk = len(kts)
                        if nk == 2:
                            P.op("act", lambda e: e.activation(out=Es[ei][:, 0:256], in_=PS[sb][:, 0:256], func=AF.Exp, scale=128 ** -0.5),
                                 reads=[PSB[sb]], writes=[EB[ei]])
                            P.op("dve" if ti % 2 == 0 else "pool", lambda e: e.tensor_tensor(out=Pm[ei][:, 0:256], in0=Es[ei][:, 0:256], in1=cb[:, CB_M1:CB_M1 + 256], op=ALU.mult),
                                 reads=[EB[ei], CBB], writes=[PMB[ei]])
                        else:
                            mcol = kts[0][1]
                            P.op("act", lambda e: e.activation(out=Es[ei][:, 0:nq], in_=PS[sb][:, 0:nq], func=AF.Exp, scale=128 ** -0.5),
                                 reads=[PSB[sb]], writes=[EB[ei]])
                            P.op("dve" if ti % 2 == 0 else "pool", lambda e: e.tensor_tensor(out=Pm[ei][:, 0:nq], in0=Es[ei][:, 0:nq], in1=cb[:, mcol + b0:mcol + b0 + nq], op=ALU.mult),
                                 reads=[EB[ei], CBB], writes=[PMB[ei]])

                    def back(ti):
                        r, tq0, nq, b0, kts = tiles[ti]
                        ub = (5, 6, 1)[ti % 3]
                        ei = ti % 4
                        nk = len(kts)
                        for ki, (kt, mcol) in enumerate(kts):
                            P.op("pe", lambda e, ki=ki, kt=kt: e.matmul(PS[ub][:, 0:nq], lhsT=vA3[:, kt, :], rhs=Pm[ei][:, ki * 128:ki * 128 + nq],
                                                                       start=(ki == 0), stop=(ki == nk - 1)),
                                 reads=VA + [PMB[ei]], writes=[PSB[ub]])
                        for ki, (kt, mcol) in enumerate(kts):
                            P.op("pe", lambda e, ki=ki: e.matmul(PS[ub][:, 128:128 + nq], lhsT=ones, rhs=Pm[ei][:, ki * 128:ki * 128 + nq],
                                                                start=(ki == 0), stop=(ki == nk - 1)),
                                 reads=[CBB, PMB[ei]], writes=[PSB[ub]])
                        if d == 1:
                            oA = acc.rearrange("p (a t) -> p a t", a=2)[:, :, tq0:tq0 + nq]
                        else:
                            oA = acc.rearrange("p (a t r) -> p a r t", a=2, r=d)[:, :, r, tq0:tq0 + nq]
                        pA = PS[ub][:, 0:256].rearrange("p (a t) -> p a t", a=2)[:, :, 0:nq]
                        if g == 0:
                            P.op("act", lambda e: e.activation(out=oA, in_=pA, func=AF.Copy), reads=[PSB[ub]], writes=[ACC[0], ACCF])
                        else:
                            P.op("dve", lambda e: e.tensor_tensor(out=oA, in0=pA, in1=oA, op=ALU.add), reads=[PSB[ub], ACC[g - 1]], writes=[ACC[g]])

                    if g < 2:
                        stage_blk(GROUPS[g + 1], ring, RINGB, 0)
                    nti = len(tiles)
                    for ti in range(nti + 3):
                        if ti < nti:
                            front(ti)
                        if ti >= 3:
                            back(ti - 3)
                slot = ws.get(n)
                n += 1

                for blk in range(8):
                    sl = slice(blk * 512, (blk + 1) * 512)
                    P.op("act", lambda e, sl=sl: e.activation(out=accD[:, sl], in_=accD[:, sl], func=AF.Ln), reads=[ACC[2]], writes=[ACCF])
                    P.op("act", lambda e, sl=sl: e.activation(out=accD[:, sl], in_=accD[:, sl], func=AF.Exp, scale=-1.0), reads=[ACCF], writes=[ACCF])

                def ev_z(blk, pi):
                    zb = blk % 2
                    sl = slice(blk * 512, (blk + 1) * 512)
                    P.op("act", lambda e: e.activation(out=szs[zb], in_=PS[pi][:, 0:512], func=AF.Silu), reads=[PSB[pi]], writes=[SZ[zb]])
                    P.op("dve", lambda e: e.tensor_tensor(out=accU[:, sl], in0=accU[:, sl], in1=accD[:, sl], op=ALU.mult),
                         reads=[ACC[2], ACCF], writes=[ACCF])
                    P.op("pool", lambda e: e.tensor_tensor(out=yast[:, sl], in0=accU[:, sl], in1=szs[zb], op=ALU.mult),
                         reads=[ACCF, SZ[zb]], writes=[YAST])

                proj_fm(slot, lambda kc, blk: rhs_class(1, kc, blk), ev_z, pjb)
                P.dma("sp", lambda q, h=h: q.dma_start(out=ya_d[h], in_=yast), yasem, reads=[YAST], writes=[YAD[h]])
            P.barrier()
            if debug and STOP == "B":
                dq = P.dma_sem()
                for nm, t in (("q", qT), ("k", kT), ("v", vA), ("oa", accU)):
                    P.dma("sp", lambda q, nm=nm, t=t: q.dma_start(out=dbg[nm], in_=t), dq, writes=[Buf("x")])
                P.barrier()
            if STOP == "B":
                return YAD, []

            state["aoff"] = a_b0
            lrT = a_bf(S)
            LRT = Buf("lrT")
            P.op("pool", lambda e: e.memset(lrT[0:64, :], 1.0), writes=[LRT])
            gup = a_bf(512)
            GUP = Buf("gup")
            gon = small[:, 200:202]
            GON = Buf("gon")
            P.dma("pool", lambda q: q.dma_start(out=gup[0:64, :], in_=gup_d[l]), msem(), writes=[GUP])
            P.dma("sp", lambda q: q.dma_start(out=gon, in_=gon_d[l]), msem(), writes=[GON])
            qbT = a_bf(S)
            kbT = a_bf(S)
            vB = a_bf(32 * 256)
            vB3 = vB.rearrange("p (t c) -> p t c", c=256)
            lg = a_bf(32 * 256)
            lg3 = lg.rearrange("p (t c) -> p t c", c=256)
            qdb = a_bf(S)
            kib = a_bf(S)
            SbA = a_bf(32 * 256)
            Sb3 = SbA.rearrange("p (t c) -> p t c", c=256)
            Sst2 = [a_f32(256) for _ in range(2)]
            Sring = [a_bf(256) for _ in range(3)]
            szb = [a_bf(1024) for _ in range(2)]
            ybst = [a_bf(1024) for _ in range(2)]
            NTMP = 3
            EWt = [a_f32(384) for _ in range(NTMP)]
            Ef = [t[:, 0:128] for t in EWt]
            Efi = [t[:, 128:256] for t in EWt]
            Wd = [t[:, 256:384] for t in EWt]
            qd = [a_bf(128) for _ in range(NTMP)]
            ki_ = [a_bf(128) for _ in range(NTMP)]
            ke = [a_bf(128) for _ in range(NTMP)]
            ABm = [a_bf(256) for _ in range(NTMP)]
            sq = [a_bf(256) for _ in range(2)]
            lnv = [a_f32(128) for _ in range(2)]
            rstd = [a_f32(128) for _ in range(2)]
            tt_ = [a_f32(256) for _ in range(2)]
            e1 = [a_f32(512) for _ in range(2)]
            QB, KB, LG, QDB, KIB, SBB = [Buf(nm) for nm in "qbT kbT lg qdb kib Sb".split()]
            SST2 = [Buf("Sst0"), Buf("Sst1")]
            VB = [Buf("vB%d" % i) for i in range(16)]
            SR = [Buf("Sr%d" % i) for i in range(3)]
            SZB = [Buf("szb0"), Buf("szb1")]
            YBST = [Buf("ybst0"), Buf("ybst1")]
            EF = [Buf("EW%d" % i) for i in range(NTMP)]
            EFI = EF
            WD = EF
            QD = [Buf("qd%d" % i) for i in range(NTMP)]
            KI = [Buf("ki%d" % i) for i in range(NTMP)]
            KE = [Buf("ke%d" % i) for i in range(NTMP)]
            AB = [Buf("AB%d" % i) for i in range(NTMP)]
            SQ = [Buf("sq0"), Buf("sq1")]
            LNV = [Buf("lnv0"), Buf("lnv1")]
            RSTD = [Buf("rstd0"), Buf("rstd1")]
            TT = [[Buf("tt%d%d" % (i, j)) for j in range(2)] for i in range(2)]
            E1 = [Buf("e10"), Buf("e11")]
            ybsem = [[msem(), msem()], [msem(), msem()]]
            YBD = [Buf("ybd%d" % i) for i in range(8)]
            slot_lr = load_wt(l, CT[("lr",)])

            def ev_lr(blk, pi):
                P.op("act", lambda e: e.activation(out=lrT[0:16, blk * 512:(blk + 1) * 512], in_=PS[pi][0:16, 0:512], func=AF.Copy),
                     reads=[PSB[pi]], writes=[LRT])
                P.op("act", lambda e: e.activation(out=lrT[32:48, blk * 512:(blk + 1) * 512], in_=PS[pi][32:48, 0:512], func=AF.Copy),
                     reads=[PSB[pi]], writes=[LRT])

            proj_fm(slot_lr, lambda kc, blk: rhs_class(1, kc, blk), ev_lr, [0, 1])
            cts = []
            for h in range(4):
                cts += [CT[("qb", h)], CT[("kb", h)], CT[("vb", h, 0)], CT[("vb", h, 1)]]
            ws = WStream(l, cts, ahead=2)
            szsem = [msem(), msem()]
            pjb = [0, 1]
            def gla_head(h):
                n = h * 4
                slot = ws.get(n)

                def ev_q(blk, pi):
                    P.op("act", lambda e: e.activation(out=qbT[:, blk * 512:(blk + 1) * 512], in_=PS[pi][:, 0:512], func=AF.Copy, scale=128 ** -0.5),
                         reads=[PSB[pi]], writes=[QB])

                proj_fm(slot, lambda kc, blk: rhs_class(1, kc, blk), ev_q, pjb)
                slot = ws.get(n + 1)

                def ev_k(blk, pi):
                    P.op("dve", lambda e: e.tensor_copy(out=kbT[:, blk * 512:(blk + 1) * 512], in_=PS[pi][:, 0:512]), reads=[PSB[pi]], writes=[KB])

                proj_fm(slot, lambda kc, blk: rhs_class(1, kc, blk), ev_k, pjb)
                def lg_step(tg):
                    eb = tg % 2
                    px = [2, 4][tg % 2]
                    py = [3, 5][tg % 2]
                    for tt in range(4):
                        tj = tg * 4 + tt
                        P.op("pe", lambda e, px=px, tt=tt, tj=tj: e.matmul(PS[px][:, tt * 128:(tt + 1) * 128], lhsT=lrT[0:32, tj * 128:(tj + 1) * 128],
                                                                         rhs=gup[0:32, h * 128:(h + 1) * 128], start=True, stop=True),
                             reads=[LRT, GUP], writes=[PSB[px]])
                        P.op("pe", lambda e, py=py, tt=tt, tj=tj: e.matmul(PS[py][:, tt * 128:(tt + 1) * 128], lhsT=lrT[32:64, tj * 128:(tj + 1) * 128],
                                                                         rhs=gup[32:64, h * 128:(h + 1) * 128], start=True, stop=True),
                             reads=[LRT, GUP], writes=[PSB[py]])
                    for di, pz in enumerate((px, py)):
                        P.op("act", lambda e, pz=pz, di=di: e.activation(out=e1[di], in_=PS[pz][:, 0:512], func=AF.Exp, scale=-1.0),
                             reads=[PSB[pz]], writes=[E1[di]])
                        P.op("act", lambda e, di=di, tg=tg: e.activation(out=lg3[:, tg * 4:(tg + 1) * 4, di * 128:(di + 1) * 128],
                                                                       in_=e1[di].rearrange("p (t c) -> p t c", c=128), func=AF.Ln, bias=1.0),
                             reads=[E1[di]], writes=[LG])
                s0 = ws.get(n + 2)
                s1 = ws.get(n + 3)

                def v_step(tg):
                    pi = next_pj(pjb)
                    for tt in range(2):
                        tj = tg * 2 + tt
                        if s1 == s0 + 1:
                            for kc in range(8):
                                P.op("pe", lambda e, pi=pi, tt=tt, tj=tj, kc=kc: e.matmul(
                                    PS[pi][:, tt * 256:(tt + 1) * 256].rearrange("p (a c) -> p a c", a=2), lhsT=hT[:, kc, tj * 128:(tj + 1) * 128],
                                    rhs=wts[:, s0:s0 + 2, kc, :], start=(kc == 0), stop=(kc == 7)),
                                    reads=[WB[s0], WB[s1], HT], writes=[PSB[pi]])
                        else:
                            for hf, sl in enumerate((s0, s1)):
                                for kc in range(8):
                                    P.op("pe", lambda e, pi=pi, tt=tt, tj=tj, hf=hf, sl=sl, kc=kc: e.matmul(
                                        PS[pi][:, tt * 256 + hf * 128:tt * 256 + hf * 128 + 128], lhsT=hT[:, kc, tj * 128:(tj + 1) * 128],
                                        rhs=wts[:, sl, kc, :], start=(kc == 0), stop=(kc == 7)),
                                        reads=[WB[sl], HT], writes=[PSB[pi]])
                    P.op("dve", lambda e, pi=pi, tg=tg: e.tensor_copy(out=vB[:, tg * 512:(tg + 1) * 512], in_=PS[pi][:, 0:512]),
                         reads=[PSB[pi]], writes=[VB[tg]])

                for i_ in range(16):
                    v_step(i_)
                    if i_ % 2 == 1:
                        lg_step(i_ // 2)
                if STOP == "C0":
                    return
                if STOP == "C1":
                    return
                P.op("pool", lambda e: e.memset(Sst2[0], 0.0), writes=[SST2[0]])
                P.op("pool", lambda e: e.memset(Sb3[:, 31, :], 0.0), writes=[SBB])

                def p1a(cn):
                    ti = cn % NTMP
                    fa = 2 + (cn % 2)
                    fb = 4 + (cn % 2)
                    cs = slice(cn * 128, (cn + 1) * 128)
                    P.op("pe", lambda e: e.matmul(PS[fa][:, 0:128], lhsT=lg3[:, cn, 128:256], rhs=cb[:, CB_TB:CB_TB + 128], start=True, stop=True),
                         reads=[LG, CBB], writes=[PSB[fa]])
                    P.op("pe", lambda e: e.matmul(PS[fa][:, 128:256], lhsT=lg3[:, cn, 128:256], rhs=cb[:, CB_TBN:CB_TBN + 128], start=True, stop=True),
                         reads=[LG, CBB], writes=[PSB[fa]])
                    P.op("pe", lambda e: e.matmul(PS[fa][:, 256:384], lhsT=cb[:, CB_SL:CB_SL + 128], rhs=lg3[:, cn, 128:256], start=True, stop=True),
                         reads=[LG, CBB], writes=[PSB[fa]])
                    kps = PS[fa][:, 384:448].bitcast(BF16)
                    if cn >= 1:
                        P.op("pe", lambda e: e.transpose(kps, kbT[:, cs], ident), reads=[KB, CBB], writes=[PSB[fa]])
                    P.op("act", lambda e: e.activation(out=EWt[ti], in_=PS[fa][:, 0:384], func=AF.Exp), reads=[PSB[fa]], writes=[EF[ti]])
                    P.op("dve", lambda e: e.tensor_tensor(out=qdb[:, cs], in0=qbT[:, cs], in1=Ef[ti], op=ALU.mult), reads=[QB, EF[ti]], writes=[QDB])
                    P.op("pool", lambda e: e.tensor_tensor(out=kib[:, cs], in0=kbT[:, cs], in1=Efi[ti], op=ALU.mult), reads=[KB, EFI[ti]], writes=[KIB])
                    if cn >= 1:
                        P.op("dve", lambda e: e.tensor_tensor(out=ke[ti], in0=kps, in1=Wd[ti], op=ALU.mult), reads=[PSB[fa], WD[ti]], writes=[KE[ti]])

                def p1b(cn):
                    if cn < 1:
                        return
                    ti = cn % NTMP
                    fb = 4 + (cn % 2)
                    src, dst = Sst2[(31 - cn) % 2], Sst2[(32 - cn) % 2]
                    SRC, DST = SST2[(31 - cn) % 2], SST2[(32 - cn) % 2]
                    P.op("pe", lambda e: e.matmul(PS[fb][:, 0:256], lhsT=ke[ti], rhs=vB3[:, cn, :], start=True, stop=True),
                         reads=[KE[ti], VB[cn // 2]], writes=[PSB[fb]])
                    P.op("dve", lambda e: e.scalar_tensor_tensor(out=dst, in0=src, scalar=Ef[ti][:, 0:1], in1=PS[fb][:, 0:256], op0=ALU.mult, op1=ALU.add),
                         reads=[SRC, EF[ti], PSB[fb]], writes=[DST])

                def p1c(cn):
                    if cn < 1:
                        return
                    P.op("act", lambda e: e.activation(out=Sb3[:, cn - 1, :], in_=Sst2[(32 - cn) % 2], func=AF.Copy), reads=[SST2[(32 - cn) % 2]], writes=[SBB])

                for cn in range(31, -3, -1):
                    if cn >= 0:
                        p1a(cn)
                    if 0 <= cn + 2 <= 31:
                        p1c(cn + 2)
                    if 0 <= cn + 1 <= 31:
                        p1b(cn + 1)
                if STOP == "C2":
                    return
                P.op("pool", lambda e: e.memset(Sst2[1], 0.0), writes=[SST2[1]])

                def s1(cn):
                    ti = cn % NTMP
                    fa = 2 + (cn % 2)
                    fb = 4 + (cn % 2)
                    cs = slice(cn * 128, (cn + 1) * 128)
                    if cn % 4 == 0:
                        blk = cn // 4
                        zb = blk % 2
                        P.dma("sp", lambda q: q.dma_start(out=szb[zb].rearrange("p (k t) -> p k t", t=512),
                                                         in_=szb_d[2 * h:2 * h + 2, :, blk * 512:(blk + 1) * 512].rearrange("k p t -> p k t")),
                              szsem[zb], reads=[SZD[2 * h], SZD[2 * h + 1]], writes=[SZB[zb]])
                    P.op("pe", lambda e: e.matmul(PS[fa][:, 0:128], lhsT=lg3[:, cn, 0:128], rhs=cb[:, CB_TF:CB_TF + 128], start=True, stop=True),
                         reads=[LG, CBB], writes=[PSB[fa]])
                    P.op("pe", lambda e: e.matmul(PS[fa][:, 128:256], lhsT=lg3[:, cn, 0:128], rhs=cb[:, CB_TFN:CB_TFN + 128], start=True, stop=True),
                         reads=[LG, CBB], writes=[PSB[fa]])
                    P.op("pe", lambda e: e.matmul(PS[fa][:, 256:384], lhsT=cb[:, CB_SU:CB_SU + 128], rhs=lg3[:, cn, 0:128], start=True, stop=True),
                         reads=[LG, CBB], writes=[PSB[fa]])
                    kps = PS[fa][:, 384:448].bitcast(BF16)
                    P.op("pe", lambda e: e.transpose(kps, kbT[:, cs], ident), reads=[KB, CBB], writes=[PSB[fa]])
                    P.op("act", lambda e: e.activation(out=EWt[ti], in_=PS[fa][:, 0:384], func=AF.Exp), reads=[PSB[fa]], writes=[EF[ti]])
                    P.op("dve", lambda e: e.tensor_tensor(out=qd[ti], in0=qbT[:, cs], in1=Ef[ti], op=ALU.mult), reads=[QB, EF[ti]], writes=[QD[ti]])
                    P.op("pool", lambda e: e.tensor_tensor(out=ki_[ti], in0=kbT[:, cs], in1=Efi[ti], op=ALU.mult), reads=[KB, EFI[ti]], writes=[KI[ti]])
                    P.op("dve", lambda e: e.tensor_tensor(out=ke[ti], in0=kps, in1=Wd[ti], op=ALU.mult), reads=[PSB[fa], WD[ti]], writes=[KE[ti]])

                def s2(cn):
                    ti = cn % NTMP
                    fa = 2 + (cn % 2)
                    fb = 4 + (cn % 2)
                    cs = slice(cn * 128, (cn + 1) * 128)
                    P.op("pe", lambda e: e.matmul(PS[fb][:, 256:384], lhsT=ki_[ti], rhs=qd[ti], start=True, stop=True), reads=[KI[ti], QD[ti]], writes=[PSB[fb]])
                    P.op("pe", lambda e: e.matmul(PS[fb][:, 384:512], lhsT=kib[:, cs], rhs=qdb[:, cs], start=True, stop=True), reads=[KIB, QDB], writes=[PSB[fb]])
                    if cn < 31:
                        P.op("pe", lambda e: e.matmul(PS[fb][:, 0:256], lhsT=ke[ti], rhs=vB3[:, cn, :], start=True, stop=True), reads=[KE[ti], VB[cn // 2]], writes=[PSB[fb]])
                    P.op("dve", lambda e: e.tensor_tensor(out=ABm[ti], in0=PS[fb][:, 256:512], in1=cb[:, CB_M2:CB_M2 + 256], op=ALU.mult),
                         reads=[PSB[fb], CBB], writes=[AB[ti]])
                    if cn < 31:
                        P.op("dve", lambda e: e.scalar_tensor_tensor(out=Sst2[cn % 2], in0=Sst2[(cn + 1) % 2], scalar=Ef[ti][:, 127:128], in1=PS[fb][:, 0:256],
                                                                     op0=ALU.mult, op1=ALU.add),
                             reads=[SST2[(cn + 1) % 2], EF[ti], PSB[fb]], writes=[SST2[cn % 2]])

                def s2c(cn):
                    if cn < 31:
                        P.op("act", lambda e: e.activation(out=Sring[(cn + 1) % 3], in_=Sst2[cn % 2], func=AF.Copy), reads=[SST2[cn % 2]], writes=[SR[(cn + 1) % 3]])

                def s3(cn):
                    ti = cn % NTMP
                    bz = 6 + (cn % 2)
                    b2 = cn % 2
                    cs = slice(cn * 128, (cn + 1) * 128)
                    blk = cn // 4
                    zb = blk % 2
                    for hf in range(2):
                        mms = [(vB3[:, cn, hf * 128:(hf + 1) * 128], ABm[ti][:, 0:128], [VB[cn // 2], AB[ti]]),
                               (vB3[:, cn, hf * 128:(hf + 1) * 128], ABm[ti][:, 128:256], [VB[cn // 2], AB[ti]])]
                        if cn > 0:
                            mms.append((Sring[cn % 3][:, hf * 128:(hf + 1) * 128], qd[ti], [SR[cn % 3], QD[ti]]))
                        if cn < 31:
                            mms.append((Sb3[:, cn, hf * 128:(hf + 1) * 128], qdb[:, cs], [SBB, QDB]))
                        for mi, (lh, rh, rd) in enumerate(mms):
                            P.op("pe", lambda e, lh=lh, rh=rh, mi=mi, hf=hf, nm=len(mms): e.matmul(PS[bz][:, hf * 128:(hf + 1) * 128], lhsT=lh, rhs=rh,
                                                                                                 start=(mi == 0), stop=(mi == nm - 1)),
                                 reads=rd, writes=[PSB[bz]])
                    P.op("act", lambda e: e.activation(out=sq[b2], in_=PS[bz][:, 0:256], func=AF.Square), reads=[PSB[bz]], writes=[SQ[b2]])

                def s4pe(cn):
                    bz = 6 + (cn % 2)
                    b2 = cn % 2
                    for hf in range(2):
                        P.op("pe", lambda e, hf=hf: e.matmul(PS[bz][:, 256:384], lhsT=ones, rhs=sq[b2][:, hf * 128:(hf + 1) * 128], start=(hf == 0), stop=(hf == 1)),
                             reads=[CBB, SQ[b2]], writes=[PSB[bz]])

                def s4(cn):
                    bz = 6 + (cn % 2)
                    b2 = cn % 2
                    blk = cn // 4
                    zb = blk % 2
                    P.op("act", lambda e: e.activation(out=lnv[b2], in_=PS[bz][:, 256:384], func=AF.Ln, scale=1.0 / 256, bias=EPS), reads=[PSB[bz]], writes=[LNV[b2]])
                    P.op("act", lambda e: e.activation(out=rstd[b2], in_=lnv[b2], func=AF.Exp, scale=-0.5), reads=[LNV[b2]], writes=[RSTD[b2]])
                    for hf in range(2):
                        P.op("dve", lambda e, hf=hf: e.scalar_tensor_tensor(out=tt_[b2][:, hf * 128:(hf + 1) * 128], in0=PS[bz][:, hf * 128:(hf + 1) * 128],
                                                                            scalar=gon[:, hf:hf + 1], in1=rstd[b2], op0=ALU.mult, op1=ALU.mult),
                             reads=[PSB[bz], RSTD[b2], GON], writes=[TT[b2][hf]])
                        c0 = hf * 512 + (cn % 4) * 128
                        P.op("pool", lambda e, hf=hf, c0=c0: e.tensor_tensor(out=ybst[zb][:, c0:c0 + 128], in0=tt_[b2][:, hf * 128:(hf + 1) * 128],
                                                                            in1=szb[zb][:, c0:c0 + 128], op=ALU.mult),
                             reads=[TT[b2][hf], SZB[zb]], writes=[YBST[zb]])
                    if cn % 4 == 3:
                        for hf in range(2):
                            P.dma("sp", lambda q, hf=hf: q.dma_start(out=yb_d[h * 2 + hf, :, blk * 512:(blk + 1) * 512], in_=ybst[zb][:, hf * 512:(hf + 1) * 512]),
                                  ybsem[zb][hf], reads=[YBST[zb]], writes=[YBD[h * 2 + hf]])

                for st in range(32 + 3):
                    if 0 <= st - 3 < 32:
                        s4pe(st - 3)
                    if st < 32:
                        s1(st)
                    if 0 <= st - 2 < 32:
                        s2c(st - 2)
                    if 0 <= st - 1 < 32:
                        s2(st - 1)
                    if 0 <= st - 3 < 32:
                        s4(st - 3)
                    if 0 <= st - 2 < 32:
                        s3(st - 2)
            for h_ in range(4):
                gla_head(h_)
            P.barrier()
            if STOP in ("C", "C0", "C1", "C2"):
                return YAD, YBD

            a_reset()
            wa = a_bf(4 * 1024)
            wb = a_bf(8 * 1024)
            wo = a_bf(8 * 1024)
            wa3 = wa.rearrange("p (k c) -> p k c", c=1024)
            wb3 = wb.rearrange("p (k c) -> p k c", c=1024)
            wo3 = wo.rearrange("p (k c) -> p k c", c=1024)
            wpost = a_f32(1024)
            WA, WBb, WO, WPOST = Buf("wa"), Buf("wb"), Buf("wo"), Buf("wpost")
            P.dma("pool", lambda q: q.dma_start(out=wa3, in_=wa_d[l]), msem(), writes=[WA])
            for k2 in range(2):
                P.dma("pool", lambda q, k2=k2: q.dma_start(out=wb3[:, k2 * 4:(k2 + 1) * 4, :], in_=wb_d[l, :, k2 * 4:(k2 + 1) * 4, :]), msem(), writes=[WBb])
                P.dma("pool", lambda q, k2=k2: q.dma_start(out=wo3[:, k2 * 4:(k2 + 1) * 4, :], in_=wo_d[l, :, k2 * 4:(k2 + 1) * 4, :]), msem(), writes=[WO])
            P.dma("sp", lambda q: q.dma_start(out=wpost, in_=npost_d[l:l + 1, :].partition_broadcast(128)), msem(), writes=[WPOST])
            NB = 256
            yab = [a_bf(4 * NB) for _ in range(2)]
            ybb = [a_bf(8 * NB) for _ in range(2)]
            sgb = [a_bf(16 * NB) for _ in range(2)]
            mrg = [a_bf(8 * NB) for _ in range(2)]
            tA = [a_f32(NB) for _ in range(2)]
            tB = [a_f32(NB) for _ in range(2)]
            xr = [a_f32(1024) for _ in range(4)]
            res = [a_f32(1024) for _ in range(2)]
            jk2s = [a_bf(512) for _ in range(4)]
            YAB = [Buf("yab0"), Buf("yab1")]
            YBB = [[Buf("ybb%d_%d" % (i, j)) for j in range(2)] for i in range(2)]
            SGB = [[Buf("sgb%d_%d" % (i, j)) for j in range(4)] for i in range(2)]
            MRG = [Buf("mrg0"), Buf("mrg1")]
            TA = [Buf("tA0"), Buf("tA1")]
            TB_ = [Buf("tB0"), Buf("tB1")]
            XR = [Buf("xr%d" % i) for i in range(4)]
            RES = [Buf("res0"), Buf("res1")]
            JK2s = [Buf("jk2_%d" % i) for i in range(4)]
            SM2s = [Buf("small20"), Buf("small21")]
            P.op("pool", lambda e: e.memset(small[:, 0:64], 0.0), writes=SM2s)
            ldsem = [[msem() for _ in range(7)] for _ in range(2)]
            xrsem = [msem(), msem(), msem(), msem()]
            ossem = [msem(), msem()]
            nblk = S // NB

            def ld_blk(blk):
                b = blk % 2
                ts = slice(blk * NB, (blk + 1) * NB)
                P.dma("sp", lambda q: q.dma_start(out=yab[b].rearrange("p (k t) -> p k t", t=NB), in_=ya_d[:, :, ts].rearrange("k p t -> p k t")),
                      ldsem[b][0], reads=YAD, writes=[YAB[b]])
                for pc in range(2):
                    P.dma("sp", lambda q, pc=pc: q.dma_start(out=ybb[b][:, pc * 4 * NB:(pc + 1) * 4 * NB].rearrange("p (k t) -> p k t", t=NB),
                                                            in_=yb_d[pc * 4:(pc + 1) * 4, :, ts].rearrange("k p t -> p k t")),
                          ldsem[b][1 + pc], reads=YBD, writes=[YBB[b][pc]])
                for pc in range(4):
                    P.dma("sp", lambda q, pc=pc: q.dma_start(out=sgb[b][:, pc * 4 * NB:(pc + 1) * 4 * NB].rearrange("p (k t) -> p k t", t=NB),
                                                            in_=sg_d[pc * 4:(pc + 1) * 4, :, ts].rearrange("k p t -> p k t")),
                          ldsem[b][3 + pc], reads=SGD, writes=[SGB[b][pc]])

            def ld_x(tok):
                xb = tok % 4
                P.dma("sp", lambda q: q.dma_start(out=xr[xb], in_=x_in[tok * 128:(tok + 1) * 128, :]), xrsem[xb], reads=[XIN], writes=[XR[xb]])

            ld_blk(0)
            ld_x(0)
            ld_x(1)
            def branch(blk):
                b = blk % 2
                ts = slice(blk * NB, (blk + 1) * NB)
                if blk + 1 < nblk:
                    ld_blk(blk + 1)
                for c in range(8):
                    pa = next_pj([0, 1, 2, 3])
                    for kc in range(4):
                        P.op("pe", lambda e, pa=pa, kc=kc, c=c, b=b: e.matmul(PS[pa][:, 0:NB], lhsT=wa3[:, kc, c * 128:(c + 1) * 128], rhs=yab[b][:, kc * NB:(kc + 1) * NB],
                                                                              start=(kc == 0), stop=(kc == 3)), reads=[WA, YAB[b]], writes=[PSB[pa]])
                    for kc in range(8):
                        P.op("pe", lambda e, pa=pa, kc=kc, c=c, b=b: e.matmul(PS[pa][:, NB:2 * NB], lhsT=wb3[:, kc, c * 128:(c + 1) * 128], rhs=ybb[b][:, kc * NB:(kc + 1) * NB],
                                                                              start=(kc == 0), stop=(kc == 7)), reads=[WBb, YBB[b][kc // 4]], writes=[PSB[pa]])
                    tb = c % 2
                    P.op("dve", lambda e, pa=pa, c=c, b=b, tb=tb: e.tensor_tensor(out=tA[tb], in0=PS[pa][:, 0:NB], in1=sgb[b][:, c * NB:(c + 1) * NB], op=ALU.mult),
                         reads=[PSB[pa], SGB[b][c // 4]], writes=[TA[tb]])
                    P.op("dve", lambda e, pa=pa, c=c, b=b, tb=tb: e.tensor_tensor(out=tB[tb], in0=PS[pa][:, NB:2 * NB], in1=sgb[b][:, (8 + c) * NB:(9 + c) * NB], op=ALU.mult),
                         reads=[PSB[pa], SGB[b][2 + c // 4]], writes=[TB_[tb]])
                    P.op("pool", lambda e, c=c, b=b, tb=tb: e.tensor_tensor(out=mrg[b][:, c * NB:(c + 1) * NB], in0=tA[tb], in1=tB[tb], op=ALU.add),
                         reads=[TA[tb], TB_[tb]], writes=[MRG[b]])
            def outp(blk):
                b = blk % 2
                for tt in range(NB // 128):
                    tok = blk * (NB // 128) + tt
                    xb = tok % 2
                    xq = tok % 4
                    if tok + 2 < NT:
                        ld_x(tok + 2)
                    pos_ = [4 + 2 * (tok % 2), 5 + 2 * (tok % 2)]
                    for half in range(2):
                        po = pos_[half]
                        for kc in range(8):
                            P.op("pe", lambda e, po=po, kc=kc, half=half, tt=tt, b=b: e.matmul(
                                PS[po][:, 0:512], lhsT=mrg[b][:, kc * NB + tt * 128:kc * NB + (tt + 1) * 128], rhs=wo3[:, kc, half * 512:(half + 1) * 512],
                                start=(kc == 0), stop=(kc == 7)), reads=[MRG[b], WO], writes=[PSB[po]])
                        P.op("act", lambda e, po=po, tok=tok, half=half: e.activation(out=jk2s[(2 * tok + half) % 4], in_=PS[po][:, 0:512], func=AF.Square,
                                                                                     accum_out=small[:, 2 * tok + half:2 * tok + half + 1]),
                             reads=[PSB[po]], writes=[JK2s[(2 * tok + half) % 4], SM2s[tok % 2]])
                    c_ss = 208 + (tok % 2) * 4
                    P.op("dve", lambda e, tok=tok, c_ss=c_ss: e.tensor_tensor(out=small[:, c_ss:c_ss + 1], in0=small[:, 2 * tok:2 * tok + 1], in1=small[:, 2 * tok + 1:2 * tok + 2], op=ALU.add),
                         reads=[SM2s[tok % 2]], writes=[SM2s[tok % 2]])
                    P.op("act", lambda e, c_ss=c_ss: e.activation(out=small[:, c_ss + 1:c_ss + 2], in_=small[:, c_ss:c_ss + 1], func=AF.Ln, scale=1.0 / D, bias=EPS),
                         reads=[SM2s[tok % 2]], writes=[SM2s[tok % 2]])
                    P.op("act", lambda e, c_ss=c_ss: e.activation(out=small[:, c_ss + 2:c_ss + 3], in_=small[:, c_ss + 1:c_ss + 2], func=AF.Exp, scale=-0.5),
                         reads=[SM2s[tok % 2]], writes=[SM2s[tok % 2]])
                    for half in range(2):
                        po = pos_[half]
                        hs = slice(half * 512, (half + 1) * 512)
                        P.op("dve", lambda e, po=po, hs=hs, xb=xb, c_ss=c_ss: e.scalar_tensor_tensor(out=res[xb][:, hs], in0=PS[po][:, 0:512], scalar=small[:, c_ss + 2:c_ss + 3],
                                                                                                  in1=wpost[:, hs], op0=ALU.mult, op1=ALU.mult),
                             reads=[PSB[po], SM2s[tok % 2], WPOST], writes=[RES[xb]])
                        P.op("pool", lambda e, hs=hs, xb=xb, xq=xq: e.tensor_tensor(out=res[xb][:, hs], in0=res[xb][:, hs], in1=xr[xq][:, hs], op=ALU.add),
                             reads=[RES[xb], XR[xq]], writes=[RES[xb]])
                    P.dma("sp", lambda q, tok=tok, xb=xb: q.dma_start(out=x_out[tok * 128:(tok + 1) * 128, :], in_=res[xb]), ossem[xb], reads=[RES[xb]], writes=[XOUT])
            for blk in range(nblk + 1):
                if blk < nblk:
                    branch(blk)
                if blk >= 1:
                    outp(blk - 1)
            P.barrier()
            return YAD, YBD

        XB0, XB1, XB2 = Buf("x_in"), Buf("x_mid"), Buf("x_out")
        nl = NL_RUN
        if nl == 1:
            yad, ybd = layer(0, x_d, y_d, XB0, XB2)
        else:
            yad, ybd = layer(0, x_d, x1_d, XB0, XB1)
            yad, ybd = layer(1, x1_d, y_d, XB1, XB2)
        if debug:
            dsem = P.dma_sem()
            DB = Buf("dbg")
            for h in range(4):
                P.dma("sp", lambda q, h=h: q.dma_start(out=dbg["ya"][h], in_=ya_d[h]), dsem, reads=yad, writes=[DB])
            for h in range(8):
                P.dma("sp", lambda q, h=h: q.dma_start(out=dbg["yb"][h], in_=yb_d[h]), dsem, reads=ybd, writes=[DB])
            for k in range(8):
                P.dma("sp", lambda q, k=k: q.dma_start(out=dbg["h"][:, k, :], in_=hT[:, k, :]), dsem, reads=[HT], writes=[DB])
            for j in range(6):
                P.dma("sp", lambda q, j=j: q.dma_start(out=dbg["t2"][j], in_=t2_d[j]), dsem, writes=[DB])
        P.barrier()
        P.emit()
    return nc


NL_RUN = NL
FULL_SYNC = False
STOP = None


def _prep(inputs):
    f32 = np.float32
    w_in = np.asarray(inputs["w_in"], f32)
    wp = np.zeros((NL, NCT, 128, 8, 128), f32)
    for l in range(NL):
        wz = np.concatenate([w_in[l], np.zeros((D, 1), f32)], axis=1)
        sel = wz[:, CT_COLS.reshape(-1)].reshape(8, 128, NCT, 128)
        wp[l] = sel.transpose(2, 1, 0, 3)
    wa = np.ascontiguousarray(np.asarray(inputs["w_branch_a"], f32).reshape(NL, 4, 128, 1024).transpose(0, 2, 1, 3))
    wb = np.ascontiguousarray(np.asarray(inputs["w_branch_b"], f32).reshape(NL, 8, 128, 1024).transpose(0, 2, 1, 3))
    wo = np.ascontiguousarray(np.asarray(inputs["w_out"], f32).reshape(NL, 8, 128, 1024).transpose(0, 2, 1, 3))
    gup = np.zeros((NL, 64, 512), f32)
    gup[:, 0:16] = np.asarray(inputs["gate_up_fwd"], f32)
    gup[:, 16] = np.asarray(inputs["gate_bias_fwd"], f32)
    gup[:, 32:48] = np.asarray(inputs["gate_up_bwd"], f32)
    gup[:, 48] = np.asarray(inputs["gate_bias_bwd"], f32)
    gon = np.ascontiguousarray(np.asarray(inputs["gla_out_norm"], f32).reshape(NL, 2, 128).transpose(0, 2, 1))
    cbc, ctab, stab = _consts()
    shared = {"wp": wp, "wa": wa, "wb": wb, "wo": wo, "gup": gup,
              "npre": np.ascontiguousarray(np.asarray(inputs["norm_pre"], f32)),
              "npost": np.ascontiguousarray(np.asarray(inputs["norm_post"], f32)),
              "gon": gon, "cb": cbc, "ctab": ctab, "stab": stab}
    return shared


def kernel(**inputs):
    x = np.ascontiguousarray(np.asarray(inputs["x"], np.float32))
    shared = _prep(inputs)
    nc = build(debug=False)
    in_maps = [dict(shared, x=x[b]) for b in range(8)]
    res = run_bass_kernel_spmd(nc, in_maps, core_ids=list(range(8)))
    return np.stack([np.asarray(r["y"], np.float32) for r in res.results], axis=0)
```
